# Optimizing a Trainium2 kernel written in Bass

```python
import jax, jax.numpy as jnp
from jax import lax
import numpy as np

D_MODEL = 2048
BATCH = 4
SEQ = 2048
DEPTH = 2
DEC_BATCH = 8
DEC_SEQ = 1
PAST_LEN = 16384
PAGE_SIZE = 128

RET_HEADS = 8
RET_DK = 128
RET_DV = 128
RET_WK = RET_HEADS * RET_DK
RET_WV = RET_HEADS * RET_DV
RET_CHUNK = 128
CONV_WIDTH = 1024
CONV_K = 3
SB_HEADS = 8
SB_DH = 128
SB_W = SB_HEADS * SB_DH
SB_BLOCK = 128
SB_BIAS_HI = -5.0
SB_BIAS_LO = -10.0
D_FF = 5632
P_DIM = 256
ROPE_THETA = 10000.0
EPS = 1e-6
IN_SPLITS = (RET_WK, RET_WK, RET_WV, CONV_WIDTH, CONV_WIDTH, CONV_WIDTH, SB_W, SB_W, SB_W, D_MODEL, D_MODEL, D_MODEL)
N_IN = RET_WK * 2 + RET_WV + CONV_WIDTH * 3 + SB_W * 3 + D_MODEL * 3

kernel_name = "hybrid_retention_shortconv_stickbreaking_decode_step"


def rmsnorm(x, g):
    xf = x.astype(jnp.float32)
    y = xf * lax.rsqrt(jnp.mean(xf * xf, axis=-1, keepdims=True) + EPS)
    return (y * g.astype(jnp.float32)).astype(x.dtype)


def swiglu(x, w_gu, w_down):
    g, u = jnp.split(x @ w_gu, 2, axis=-1)
    return (jax.nn.silu(g) * u) @ w_down


def rope(x, pos):
    half = x.shape[-1] // 2
    inv = ROPE_THETA ** (-jnp.arange(half, dtype=jnp.float32) / half)
    ang = pos.astype(jnp.float32)[:, None] * inv[None, :]
    cos = jnp.cos(ang)[None, :, None, :]
    sin = jnp.sin(ang)[None, :, None, :]
    xf = x.astype(jnp.float32)
    x1, x2 = xf[..., :half], xf[..., half:]
    return jnp.concatenate([x1 * cos - x2 * sin, x2 * cos + x1 * sin], axis=-1)


def split_in(h):
    parts, start = [], 0
    for width in IN_SPLITS:
        parts.append(h[..., start:start + width])
        start += width
    return parts


def ret_log_gamma():
    return jnp.log1p(-jnp.exp2(-5.0 - jnp.arange(RET_HEADS, dtype=jnp.float32)))


def retention_chunk(s, qkv):
    q, k, v = qkv
    c = q.shape[1]
    lg = ret_log_gamma()
    i = jnp.arange(c, dtype=jnp.float32)
    diff = i[:, None] - i[None, :]
    dmask = jnp.where(diff[None] >= 0, jnp.exp(lg[:, None, None] * jnp.maximum(diff, 0.0)[None]), 0.0)
    scores = jnp.einsum('bihd,bjhd->bhij', q, k) * dmask[None]
    inner = jnp.einsum('bhij,bjhe->bihe', scores, v)
    q_decay = jnp.exp(lg[None, :] * (i + 1.0)[:, None])
    cross = jnp.einsum('bihd,bhde->bihe', q, s) * q_decay[None, :, :, None]
    k_decay = jnp.exp(lg[None, :] * (c - 1.0 - i)[:, None])
    s_new = jnp.exp(lg * c)[None, :, None, None] * s + jnp.einsum('bjhd,bjhe->bhde', k * k_decay[None, :, :, None], v)
    return s_new, inner + cross


def retention(q, k, v, s0):
    b, l = q.shape[:2]
    c = RET_CHUNK if l % RET_CHUNK == 0 else l
    nc = l // c

    def to_chunks(t):
        return t.reshape(b, nc, c, *t.shape[2:]).swapaxes(0, 1)

    s, o = lax.scan(retention_chunk, s0, (to_chunks(q), to_chunks(k), to_chunks(v)))
    return o.swapaxes(0, 1).reshape(b, l, RET_HEADS, RET_DV), s


def short_conv(u, prev, w):
    ext = jnp.concatenate([prev.astype(u.dtype), u], axis=1)
    l = u.shape[1]
    out = sum(ext[:, j:j + l] * w[j] for j in range(CONV_K))
    return out, ext[:, -(CONV_K - 1):]


def sb_block(q, k, v, q_pos, k_pos, bias):
    z = jnp.einsum('bqhd,bkhd->bhqk', q, k) * (SB_DH ** -0.5) + bias.astype(jnp.float32)[None, :, None, None]
    mask = (k_pos[None, :] < q_pos[:, None])[None, None]
    log_1mb = jnp.where(mask, jax.nn.log_sigmoid(-z), 0.0)
    after = lax.cumsum(log_1mb, axis=3, reverse=True) - log_1mb
    a = jnp.where(mask, jnp.exp(jax.nn.log_sigmoid(z) + after), 0.0)
    return jnp.einsum('bhqk,bkhd->bqhd', a, v)


def stick_breaking(q, k, v, q_pos, k_pos, bias):
    b, lq = q.shape[:2]
    t = SB_BLOCK if lq % SB_BLOCK == 0 else lq
    nb = lq // t
    qb = q.reshape(b, nb, t, SB_HEADS, SB_DH).swapaxes(0, 1)
    pb = q_pos.reshape(nb, t)
    o = lax.map(lambda a: sb_block(a[0], k, v, a[1], k_pos, bias), (qb, pb))
    return o.swapaxes(0, 1).reshape(b, lq, SB_HEADS, SB_DH)


def trunk_layer(x, p, offset, s_ret, conv_prev, past_k, past_v, w):
    b, l, _ = x.shape
    f32 = jnp.float32
    pos = offset + jnp.arange(l, dtype=jnp.int32)
    x = x + 0.5 * swiglu(rmsnorm(x, w['ffn1_norm']), w['ffn1_w_gu'], w['ffn1_w_down'])
    xn = rmsnorm(x, w['mix_norm'])
    rq, rk, rv, ch, cb, cc, sq, sk, sv, ga, gb, gc = split_in(xn @ w['w_in'])
    rq = rope(rq.reshape(b, l, RET_HEADS, RET_DK), pos)
    rk = rope(rk.reshape(b, l, RET_HEADS, RET_DK), pos) * (RET_DK ** -0.5)
    rv = rv.reshape(b, l, RET_HEADS, RET_DV).astype(f32)
    ro, s_new = retention(rq, rk, rv, s_ret.astype(f32))
    ya = rmsnorm(ro, w['ret_gn']).reshape(b, l, RET_WV).astype(x.dtype)
    cv, conv_new = short_conv(cc * ch, conv_prev, w['conv_w'])
    yb = cb * cv
    sq = rmsnorm(sq.reshape(b, l, SB_HEADS, SB_DH), w['sb_q_norm'])
    sk = rmsnorm(sk.reshape(b, l, SB_HEADS, SB_DH), w['sb_k_norm'])
    sv = sv.reshape(b, l, SB_HEADS, SB_DH)
    if past_k is None:
        k_all, v_all = sk, sv
    else:
        k_all = jnp.concatenate([past_k.astype(sk.dtype), sk], axis=1)
        v_all = jnp.concatenate([past_v.astype(sv.dtype), sv], axis=1)
    k_pos = jnp.arange(k_all.shape[1], dtype=jnp.int32)
    yc = stick_breaking(sq.astype(f32), k_all.astype(f32), v_all.astype(f32), pos, k_pos, w['sb_bias'])
    yc = yc.reshape(b, l, SB_W).astype(x.dtype)
    merged = (jax.nn.sigmoid(ga) * (ya @ w['w_branch_ret'])
              + jax.nn.sigmoid(gb) * (yb @ w['w_branch_conv'])
              + jax.nn.sigmoid(gc) * (yc @ w['w_branch_sb']))
    x = x + merged @ w['w_out']
    x = x + 0.5 * swiglu(rmsnorm(x, w['ffn2_norm']), w['ffn2_w_gu'], w['ffn2_w_down'])
    x = x + jax.nn.sigmoid(rmsnorm(x, w['ple_norm']) @ w['w_ple_gate']) * (p @ w['w_ple_up'])
    return x, s_new, conv_new, sk, sv


def setup_inputs(seed: int = 0) -> dict:
    key = jax.random.key(seed)
    ks = jax.random.split(key, 32)
    f32 = jnp.float32
    n_pages = PAST_LEN // PAGE_SIZE
    n_used = DEC_BATCH * n_pages
    n_pool = n_used + (n_used + 3) // 4

    def nrm(k, shape, fan_in):
        return jax.random.normal(k, shape, f32) * (fan_in ** -0.5)

    def gain(k, shape):
        return 1.0 + 0.02 * jax.random.normal(k, shape, f32)

    perm = jax.random.permutation(ks[7], n_pool)
    page_table = perm[:n_used].reshape(DEC_BATCH, n_pages).astype(jnp.int32)
    sb_bias = jnp.linspace(SB_BIAS_HI, SB_BIAS_LO, SB_HEADS, dtype=f32)[None, :] + 0.1 * jax.random.normal(ks[28], (DEPTH, SB_HEADS), f32)
    return {
        'x_prompt': jax.random.normal(ks[0], (BATCH, SEQ, D_MODEL), f32),
        'x_sample': jax.random.normal(ks[1], (DEC_BATCH, DEC_SEQ, D_MODEL), f32),
        'cache_sb_k': jax.random.normal(ks[2], (DEPTH, n_pool, PAGE_SIZE, SB_HEADS, SB_DH), f32),
        'cache_sb_v': jax.random.normal(ks[3], (DEPTH, n_pool, PAGE_SIZE, SB_HEADS, SB_DH), f32),
        'state_ret': 0.3 * jax.random.normal(ks[4], (DEPTH, DEC_BATCH, RET_HEADS, RET_DK, RET_DV), f32),
        'state_conv': jax.random.normal(ks[5], (DEPTH, DEC_BATCH, CONV_K - 1, CONV_WIDTH), f32),
        'page_table': page_table,
        'p_prompt': jax.random.normal(ks[6], (DEPTH, BATCH, SEQ, P_DIM), f32),
        'p_sample': jax.random.normal(ks[8], (DEPTH, DEC_BATCH, DEC_SEQ, P_DIM), f32),
        'ffn1_norm': gain(ks[9], (DEPTH, D_MODEL)),
        'ffn1_w_gu': nrm(ks[10], (DEPTH, D_MODEL, 2 * D_FF), D_MODEL),
        'ffn1_w_down': nrm(ks[11], (DEPTH, D_FF, D_MODEL), D_FF),
        'mix_norm': gain(ks[12], (DEPTH, D_MODEL)),
        'w_in': nrm(ks[13], (DEPTH, D_MODEL, N_IN), D_MODEL),
        'ret_gn': gain(ks[14], (DEPTH, RET_HEADS, RET_DV)),
        'conv_w': nrm(ks[15], (DEPTH, CONV_K, CONV_WIDTH), CONV_K),
        'sb_q_norm': gain(ks[16], (DEPTH, SB_DH)),
        'sb_k_norm': gain(ks[17], (DEPTH, SB_DH)),
        'sb_bias': sb_bias,
        'w_branch_ret': nrm(ks[18], (DEPTH, RET_WV, D_MODEL), RET_WV),
        'w_branch_conv': nrm(ks[19], (DEPTH, CONV_WIDTH, D_MODEL), CONV_WIDTH),
        'w_branch_sb': nrm(ks[20], (DEPTH, SB_W, D_MODEL), SB_W),
        'w_out': nrm(ks[21], (DEPTH, D_MODEL, D_MODEL), D_MODEL),
        'ffn2_norm': gain(ks[22], (DEPTH, D_MODEL)),
        'ffn2_w_gu': nrm(ks[23], (DEPTH, D_MODEL, 2 * D_FF), D_MODEL),
        'ffn2_w_down': nrm(ks[24], (DEPTH, D_FF, D_MODEL), D_FF),
        'ple_norm': gain(ks[25], (DEPTH, D_MODEL)),
        'w_ple_gate': nrm(ks[26], (DEPTH, D_MODEL, D_MODEL), D_MODEL),
        'w_ple_up': nrm(ks[27], (DEPTH, P_DIM, D_MODEL), P_DIM),
    }


def reference(x_prompt, x_sample, cache_sb_k, cache_sb_v, state_ret, state_conv, page_table, p_prompt, p_sample,
              ffn1_norm, ffn1_w_gu, ffn1_w_down, mix_norm, w_in, ret_gn, conv_w, sb_q_norm, sb_k_norm, sb_bias,
              w_branch_ret, w_branch_conv, w_branch_sb, w_out, ffn2_norm, ffn2_w_gu, ffn2_w_down,
              ple_norm, w_ple_gate, w_ple_up):
    n_pages = PAST_LEN // PAGE_SIZE
    b_p = x_prompt.shape[0]
    b_s = x_sample.shape[0]
    yp, ys = x_prompt, x_sample
    kp_l, vp_l, ks_l, vs_l, rp_l, rs_l, cp_l, cs_l = [], [], [], [], [], [], [], []
    for i in range(DEPTH):
        w = {
            'ffn1_norm': ffn1_norm[i], 'ffn1_w_gu': ffn1_w_gu[i], 'ffn1_w_down': ffn1_w_down[i],
            'mix_norm': mix_norm[i], 'w_in': w_in[i], 'ret_gn': ret_gn[i], 'conv_w': conv_w[i],
            'sb_q_norm': sb_q_norm[i], 'sb_k_norm': sb_k_norm[i], 'sb_bias': sb_bias[i],
            'w_branch_ret': w_branch_ret[i], 'w_branch_conv': w_branch_conv[i], 'w_branch_sb': w_branch_sb[i],
            'w_out': w_out[i], 'ffn2_norm': ffn2_norm[i], 'ffn2_w_gu': ffn2_w_gu[i], 'ffn2_w_down': ffn2_w_down[i],
            'ple_norm': ple_norm[i], 'w_ple_gate': w_ple_gate[i], 'w_ple_up': w_ple_up[i],
        }
        s0 = jnp.zeros((b_p, RET_HEADS, RET_DK, RET_DV), jnp.float32)
        c0 = jnp.zeros((b_p, CONV_K - 1, CONV_WIDTH), yp.dtype)
        yp, sp, cp, kp, vp = trunk_layer(yp, p_prompt[i], 0, s0, c0, None, None, w)
        past_k = cache_sb_k[i][page_table].reshape(b_s, n_pages * PAGE_SIZE, SB_HEADS, SB_DH)
        past_v = cache_sb_v[i][page_table].reshape(b_s, n_pages * PAGE_SIZE, SB_HEADS, SB_DH)
        ys, ss, cs, ksm, vsm = trunk_layer(ys, p_sample[i], PAST_LEN, state_ret[i], state_conv[i], past_k, past_v, w)
        kp_l.append(kp); vp_l.append(vp); ks_l.append(ksm); vs_l.append(vsm)
        rp_l.append(sp); rs_l.append(ss); cp_l.append(cp); cs_l.append(cs)
    return (yp, ys, jnp.stack(kp_l), jnp.stack(vp_l), jnp.stack(ks_l), jnp.stack(vs_l),
            jnp.stack(rp_l), jnp.stack(rs_l), jnp.stack(cp_l), jnp.stack(cs_l))
```

```python
import contextlib
import os
import numpy as np
import ml_dtypes
import concourse.bass as bass
import concourse.mybir as mybir
from concourse.bass_utils import run_bass_kernel_spmd

F32 = mybir.dt.float32
BF16 = mybir.dt.bfloat16
I32 = mybir.dt.int32
AF = mybir.ActivationFunctionType
ALU = mybir.AluOpType
AX = mybir.AxisListType

D = 2048
L = 2
NP = 1024
NS = 8
NT = NP + NS
DFF = 5632
NIN = 15360
H = 8
DH = 128
PAST = 16384
NPAGE = 128
NPOOL = 1280
EPS = 1e-6
CTS = [(0, 512), (512, 512), (1024, 8)]
TCS = [(i * 128, 128) for i in range(8)] + [(1024, 8)]
GAMMA = [1.0 - 2.0 ** (-5.0 - h) for h in range(H)]
STOP_AFTER = None
DEBUG_SMALL = False
ENABLE_MIXER = True


class Dep:
    __slots__ = ("w", "r", "name", "excl")

    def __init__(self, name="", excl=False):
        self.w = []
        self.r = []
        self.name = name
        self.excl = excl


class Ctx:
    def __init__(self, nc, es):
        self.nc = nc
        self.es = es
        self.engs = {}
        self.dsems = []
        self.dnext = 0
        self.trace = {}
        self.tag = ""

    def add_engine(self, name, handle):
        sem = self.es.enter_context(self.nc.semaphore("sem_" + name))
        self.engs[name] = {"h": handle, "sem": sem, "count": 0, "wm": {}, "name": name}

    def add_dma_sems(self, n):
        self.rings = {"sp": [], "pool": []}
        self.rnext = {"sp": 0, "pool": 0}
        for i in range(n):
            sem = self.es.enter_context(self.nc.semaphore("dsem%d" % i))
            slot = [sem, 0]
            self.dsems.append(slot)
            self.rings["sp" if i < (2 * n) // 3 else "pool"].append(slot)

    def _wait(self, e, events):
        for (sem, val) in events:
            if e["name"] == "pe" and sem is e["sem"]:
                continue
            key = id(sem)
            if e["wm"].get(key, 0) < val:
                e["h"].wait_ge(sem, val)
                e["wm"][key] = val
                self.trace.setdefault(e["name"], []).append(("wait", key, val, self.tag))

    @staticmethod
    def _events(reads, writes):
        ev = []
        for d in reads:
            ev.extend(d.w)
            if d.excl:
                ev.extend(d.r)
        for d in writes:
            ev.extend(d.w)
            ev.extend(d.r)
        return ev

    def op(self, en, fn, reads=(), writes=(), inc=True):
        e = self.engs[en]
        self._wait(e, self._events(reads, writes))
        val = e["count"] + 1
        inst = fn(e["h"])
        if inc:
            inst.then_inc(e["sem"], 1)
            e["count"] = val
            self.trace.setdefault(e["name"], []).append(("inc", id(e["sem"]), 1, self.tag))
        ev = (e["sem"], val)
        for d in reads:
            d.r.append(ev)
        for d in writes:
            d.w = [ev]
            d.r = []
        return inst

    def dma(self, qn, out, in_, reads=(), writes=(), more=False, fn=None):
        e = self.engs[qn]
        if more:
            ev = []
            for d in reads:
                ev.extend(d.w)
        else:
            ev = self._events(reads, writes)
        self._wait(e, ev)
        ring = self.rings[qn]
        slot = ring[self.rnext[qn]]
        self.rnext[qn] = (self.rnext[qn] + 1) % len(ring)
        self._wait(e, [(slot[0], slot[1])])
        if fn is None:
            inst = e["h"].dma_start(out=out, in_=in_)
        else:
            inst = fn(e["h"])
        slot[1] += 16
        inst.then_inc(slot[0], 16)
        self.trace.setdefault(e["name"], []).append(("inc", id(slot[0]), 16, self.tag))
        evn = (slot[0], slot[1])
        for d in reads:
            d.r.append(evn)
        for d in writes:
            if more:
                d.w.append(evn)
            else:
                d.w = [evn]
                d.r = []
        return inst


def build(stop_after=None):
    nc = bass.Bass("TRN2", target_bir_lowering=False)
    es = contextlib.ExitStack()

    def din(name, shape, dt=F32):
        return nc.dram_tensor(name, list(shape), dt, kind="ExternalInput").ap()

    def dout(name, shape, dt=F32):
        return nc.dram_tensor(name, list(shape), dt, kind="ExternalOutput").ap()

    def dscr(name, shape, dt):
        return nc.dram_tensor(name, list(shape), dt)

    x_in = din("x_in", [NT, D])
    p_in = din("p_in", [L, NT, 256])
    ck = din("ck", [L, 8 if DEBUG_SMALL else NPOOL, 128, DH])
    cv = din("cv", [L, 8 if DEBUG_SMALL else NPOOL, 128, DH])
    st_ret = din("st_ret", [L, NS, H, DH, DH])
    st_conv = din("st_conv", [L, NS * 2, 1024])
    ptab = din("ptab", [NS, NPAGE], I32)
    if DEBUG_SMALL:
        w_gu = [din("ffn1_w_gu", [L, 1, 1]), din("ffn2_w_gu", [L, 1, 1])]
        w_dn = [din("ffn1_w_down", [L, 1, 1]), din("ffn2_w_down", [L, 1, 1])]
    else:
        w_gu = [din("ffn1_w_gu", [L, D, 2 * DFF]), din("ffn2_w_gu", [L, D, 2 * DFF])]
        w_dn = [din("ffn1_w_down", [L, DFF, D]), din("ffn2_w_down", [L, DFF, D])]
    w_in = din("w_in", [L, D, NIN])
    SMALLW = DEBUG_SMALL == 2
    w_br = [din(n_, [L, 1, 1] if SMALLW else [L, 1024, D]) for n_ in ("w_branch_ret", "w_branch_conv", "w_branch_sb")]
    w_out = din("w_out", [L, 1, 1] if SMALLW else [L, D, D])
    w_pg = din("w_ple_gate", [L, 1, 1] if SMALLW else [L, D, D])
    w_pu = din("w_ple_up", [L, 256, D])
    gam_in = din("gam", [128, L * 4 * 16])
    retgn_in = din("ret_gn", [L, 1024])
    sbq_in = din("sb_q_norm", [L, 128])
    sbk_in = din("sb_k_norm", [L, 128])
    sbb_in = din("sb_bias", [L, H])
    convw_in = din("convw", [128, L * 8 * 3])
    flag_in = din("flag", [128, 1])
    c_idf = din("c_idf", [128, 128])
    c_idb = din("c_idb", [128, 128], BF16)
    c_ones = din("c_ones", [128, 128], BF16)
    c_m01 = din("c_m01", [128, 128])
    c_msb = din("c_msb", [128, 4 * 512])
    c_nu = din("c_nu", [128, 128], BF16)
    c_nl = din("c_nl", [128, 128], BF16)
    c_usf = din("c_usf", [128, 128])
    c_cos = din("c_cos", [128, 9 * 64])
    c_sin = din("c_sin", [128, 9 * 64])
    c_dec = din("c_dec", [128, 4 * 9 * 8])
    c_oh = din("c_oh", [128, 64])

    y_out = dout("y", [NT, D])
    nk_out = dout("nk", [L, NT, 1024])
    nv_out = dout("nv", [L, NT, 1024])
    rs_out = dout("rs", [L, H, DH, DH])
    rss_out = dout("rss", [L, NS, H, DH, DH])
    cs_out = dout("cs", [L, 2, 1024])
    css_out = dout("css", [L, NS * 2, 1024])

    qt_scr = dscr("qt_scr", [L, H * 128, NT], BF16)
    rqt = dscr("rqt", [L, 9, 128, 1024], BF16)
    rkt = dscr("rkt", [L, 9, 128, 1024], BF16)
    rkd = dscr("rkd", [L, 9, 128, 1024], BF16)
    rktot = dscr("rktot", [L, 9, 128, 1024], BF16)
    rvs = dscr("rvs", [L, 9, 128, 1024], BF16)
    mg_scr = dscr("mg_scr", [L, 16, 128, NT], BF16)
    qs_scr = dscr("qs_scr", [L, NS, 128], F32)


    def sb(name, shape, dt):
        return es.enter_context(nc.sbuf_tensor(name, list(shape), dt))

    K = Ctx(nc, es)
    es.enter_context(nc.allow_non_contiguous_dma(reason="small strided layouts"))
    K.add_engine("pe", nc.tensor)
    K.add_engine("act", nc.scalar)
    K.add_engine("dve", nc.vector)
    K.add_engine("pool", nc.gpsimd)
    K.add_engine("sp", nc.sync)
    K.add_dma_sems(24)
    ccsem = es.enter_context(nc.semaphore("ccsem"))
    cccount = [0]
    ccevs = []

    def barrier():
        evs = [(e["sem"], e["count"]) for e in K.engs.values()] + [(s[0], s[1]) for s in K.dsems]
        evs.extend(ccevs)
        for e in K.engs.values():
            K._wait(e, evs)

    X = sb("X", [128, 16 * NT], F32)
    XN = sb("XN", [128, 16 * NT], BF16)
    BIG = sb("BIG", [128, 11 * NT], BF16)
    WS = [sb("WS0", [128, 4096], BF16), sb("WS1", [128, 4096], BF16)]
    WSD = [Dep("ws0"), Dep("ws1")]
    T = [sb("T%d" % i, [128, NT], F32) for i in range(4)]
    TD = [Dep("t%d" % i) for i in range(4)]
    SQ = [sb("SQ%d" % i, [128, 512], BF16) for i in range(2)]
    SQB = [sb("SQB%d" % i, [128, 512], BF16) for i in range(4)]
    SQBD = [Dep("sqb%d" % i) for i in range(4)]
    SQD = [Dep(), Dep()]
    PS = [es.enter_context(nc.psum_tensor("PS%d" % i, [128, 512], F32)) for i in range(8)]
    PD = [Dep("ps%d" % i, excl=True) for i in range(8)]
    IDF = sb("IDF", [128, 128], F32)
    IDB = sb("IDB", [128, 128], BF16)
    ONES = sb("ONES", [128, 128], BF16)
    M01 = sb("M01", [128, 128], F32)
    MSB = sb("MSB", [128, 2048], F32)
    NU = sb("NU", [128, 128], BF16)
    NL = sb("NL", [128, 128], BF16)
    USF = sb("USF", [128, 128], F32)
    COS = sb("COS", [128, 9 * 64], F32)
    SIN = sb("SIN", [128, 9 * 64], F32)
    DEC = sb("DEC", [128, 4 * 72], F32)
    OH = sb("OH", [128, 64], F32)
    GAM = sb("GAM", [128, L * 64], F32)
    CONVW = sb("CONVW", [128, L * 24], F32)
    FLAG = sb("FLAG", [128, 1], F32)
    RETGN = sb("RETGN", [128, 1024], F32)
    SBQ = sb("SBQ", [128, 128], F32)
    SBK = sb("SBK", [128, 128], F32)
    SBB = sb("SBB", [128, H], F32)
    SBBP = sb("SBBP", [128, H], F32)
    SM = sb("SM", [128, 64], F32)
    SMD = Dep("sm")
    CD = Dep("consts")

    Xv = X[:].rearrange("p (c t) -> p c t", c=16)
    XNv = XN[:].rearrange("p (c t) -> p c t", c=16)
    XD = [Dep("x%d" % i) for i in range(16)]
    XND = Dep("xn")
    BIGD = Dep("big")

    def mm(out, lhsT, rhs, start, stop, reads, writes, inc=None):
        return K.op("pe", lambda e: e.matmul(out, lhsT=lhsT, rhs=rhs, start=start, stop=stop), reads, writes,
                    inc=True if inc is None else inc)

    def tr(out, in_, ident, reads, writes):
        return K.op("pe", lambda e: e.transpose(out, in_, ident), reads, writes)

    def act(out, in_, func, reads, writes, scale=None, bias=None, accum=None):
        kw = {}
        if scale is not None:
            kw["scale"] = scale
        if bias is not None:
            kw["bias"] = bias
        if accum is not None:
            kw["accum_out"] = accum
        return K.op("act", lambda e: e.activation(out=out, in_=in_, func=func, **kw), reads, writes)

    def tt(out, in0, in1, op, reads, writes, en="dve"):
        return K.op(en, lambda e: e.tensor_tensor(out=out, in0=in0, in1=in1, op=op), reads, writes)

    def stt(out, in0, scalar, in1, op0, op1, reads, writes):
        return K.op("dve", lambda e: e.scalar_tensor_tensor(out=out, in0=in0, scalar=scalar, in1=in1, op0=op0, op1=op1),
                    reads, writes)

    def ts(out, in0, s1, op0, reads, writes, s2=None, op1=None, en="dve"):
        if op1 is None:
            return K.op(en, lambda e: e.tensor_scalar(out=out, in0=in0, scalar1=s1, scalar2=None, op0=op0), reads, writes)
        return K.op(en, lambda e: e.tensor_scalar(out=out, in0=in0, scalar1=s1, scalar2=s2, op0=op0, op1=op1), reads, writes)

    def cp(out, in_, reads, writes, en="dve"):
        if en == "act":
            return act(out, in_, AF.Copy, reads, writes)
        return K.op(en, lambda e: e.tensor_copy(out, in_), reads, writes)

    slab_i = [0]

    def load_slab(wap, r0, nk, c0, ncols, off=0, more=False, same=False):
        if not (more or same):
            slab_i[0] ^= 1
        i = slab_i[0]
        v = WS[i][:, off:off + nk * ncols].rearrange("p (k n) -> p k n", k=nk)
        src = wap[r0:r0 + nk * 128, c0:c0 + ncols].rearrange("(k p) n -> p k n", p=128)
        K.dma("pool", v, src, writes=[WSD[i]], more=more)
        return v, WSD[i]

    for (t_, a_) in [(IDF, c_idf), (IDB, c_idb), (ONES, c_ones), (M01, c_m01), (MSB, c_msb), (NU, c_nu), (NL, c_nl),
                     (USF, c_usf), (COS, c_cos), (SIN, c_sin), (DEC, c_dec), (OH, c_oh), (GAM, gam_in),
                     (CONVW, convw_in), (FLAG, flag_in)]:
        K.dma("sp", t_[:], a_[:, :], writes=[CD], more=True)
    COSv = COS[:].rearrange("p (c e) -> p c e", c=9)
    SINv = SIN[:].rearrange("p (c e) -> p c e", c=9)
    DECv = DEC[:].rearrange("p (k c h) -> p k c h", k=4, c=9)
    DINV, DDEC, DTOT, DQ = 0, 1, 2, 3

    BIGF = BIG[:].bitcast(F32)
    for ti, (t0, r) in enumerate(TCS):
        stg = BIGF[:, (ti % 2) * 2048:(ti % 2) * 2048 + 2048]
        sd = SQD[ti % 2]
        K.dma("sp", stg[:r, :], x_in[t0:t0 + r, :], writes=[sd])
        for g in range(4):
            pb = (ti * 4 + g) % 4
            for j in range(4):
                fc = g * 4 + j
                tr(PS[pb][:, j * 128:j * 128 + r], stg[:r, fc * 128:(fc + 1) * 128], IDF[:r, :r], [sd, CD], [PD[pb]])
            cp(Xv[:, g * 4:g * 4 + 4, t0:t0 + r], PS[pb][:, :].rearrange("p (j t) -> p j t", j=4)[:, :, :r],
               [PD[pb]], [XD[g * 4 + j] for j in range(4)], en=("act" if g % 2 else "dve"))

    def rmsnorm(gidx):
        for (c0, n) in CTS:
            for fc in range(16):
                act(SQ[fc % 2][:, :n], Xv[:, fc, c0:c0 + n], AF.Square, [XD[fc]], [SQD[fc % 2]])
                mm(PS[7][:, :n], ONES[:], SQ[fc % 2][:, :n], fc == 0, fc == 15, [SQD[fc % 2], CD], [PD[7]])
            act(T[0][:, :n], PS[7][:, :n], AF.Ln, [PD[7]], [TD[0]], scale=1.0 / D, bias=EPS)
            act(T[1][:, :n], T[0][:, :n], AF.Exp, [TD[0]], [TD[1]], scale=-0.5)
            for fc in range(16):
                stt(XNv[:, fc, c0:c0 + n], Xv[:, fc, c0:c0 + n], GAM[:, gidx * 16 + fc:gidx * 16 + fc + 1], T[1][:, :n],
                    ALU.mult, ALU.mult, [XD[fc], TD[1], CD], [XND])

    HBv = BIG[:].rearrange("p (i t) -> p i t", i=11)

    def ffn(l, which):
        if DEBUG_SMALL:
            return
        wgu = w_gu[which][l]
        wdn = w_dn[which][l]
        it = 0
        for q4 in range(4):
            for i in range(11):
                hc = q4 * 11 + i
                sv, sdp = load_slab(wgu, 0, 16, hc * 128, 128)
                load_slab(wgu, 0, 16, DFF + hc * 128, 128, off=2048, more=True)
                uv = WS[slab_i[0]][:, 2048:4096].rearrange("p (k n) -> p k n", k=16)
                for (c0, n) in CTS:
                    pg, pu = (it % 2) * 2, (it % 2) * 2 + 1
                    it += 1
                    for kc in range(16):
                        mm(PS[pg][:, :n], sv[:, kc, :], XNv[:, kc, c0:c0 + n], kc == 0, kc == 15, [sdp, XND], [PD[pg]], inc=(kc == 15))
                    for kc in range(16):
                        mm(PS[pu][:, :n], uv[:, kc, :], XNv[:, kc, c0:c0 + n], kc == 0, kc == 15, [sdp, XND], [PD[pu]], inc=(kc == 15))
                    tq = it % 2
                    act(T[tq][:, :n], PS[pg][:, :n], AF.Silu, [PD[pg]], [TD[tq]])
                    tt(HBv[:, i, c0:c0 + n], T[tq][:, :n], PS[pu][:, :n], ALU.mult, [TD[tq], PD[pu]], [BIGD])
            for og in range(8):
                dv, ddp = load_slab(wdn, q4 * 1408, 11, og * 256, 256)
                for o2 in range(2):
                    oc = og * 2 + o2
                    for (c0, n) in CTS:
                        pdn = 4 + (it % 2)
                        it += 1
                        for i in range(11):
                            mm(PS[pdn][:, :n], dv[:, i, o2 * 128:(o2 + 1) * 128], HBv[:, i, c0:c0 + n], i == 0, i == 10,
                               [ddp, BIGD], [PD[pdn]], inc=(i == 10))
                        stt(Xv[:, oc, c0:c0 + n], PS[pdn][:, :n], 0.5, Xv[:, oc, c0:c0 + n], ALU.mult, ALU.add,
                            [PD[pdn]], [XD[oc]])

    XB = X[:].bitcast(BF16)
    YAv = XB[:, 0:8 * NT].rearrange("p (c t) -> p c t", c=8)
    YBv = XB[:, 8 * NT:16 * NT].rearrange("p (c t) -> p c t", c=8)
    YCv = XB[:, 16 * NT:24 * NT].rearrange("p (c t) -> p c t", c=8)
    ATT = XB[:, 24 * NT:32 * NT]
    ATTF = X[:, 12 * NT:16 * NT]
    YAD, YBD, YCD = Dep("ya"), Dep("yb"), Dep("yc")
    KT2 = BIG[:, 0:2 * NT].rearrange("p (h t) -> p h t", h=2)
    KT2D = Dep("kt2")
    SK = BIGF[:, 1032:2056]
    SV = BIGF[:, 2056:3080]
    SQT = BIGF[:, 3080:3144].rearrange("p (h s) -> p h s", h=8)
    SKD, SVD, SQTD = Dep("sk"), Dep("sv"), Dep("sqt")
    RB = BIG[:, 6288:6288 + 4096]
    RBD = [Dep("rb%d" % i) for i in range(4)]
    UBf = BIGF[:, 5192:5192 + 484]
    UB = sb("UB", [128, NT + 2], F32)
    UBD = Dep("ub")
    IDX = sb("IDX", [128, 8], I32)
    IDXALL = sb("IDXALL", [128, 64], I32)
    OHC = sb("OHC", [128, 8], F32)
    BOWN = sb("BOWN", [128, 1], F32)
    QSEL = sb("QSEL", [128, 128], F32)
    QSELD = Dep("qsel")
    SM2 = sb("SM2", [128, 64], F32)
    SM3 = sb("SM3", [128, 64], F32)
    SM2D, SM3D = Dep("sm2"), Dep("sm3")
    x_scr = dscr("x_scr", [128, 16 * NT], F32)
    XSD = Dep("xscr")
    CC1D, CC2D, CC3D, QTD, RQD, MGD, QSD = Dep(), Dep(), Dep(), Dep(), Dep(), Dep(), Dep()
    CC1O, CC2O, CC3O, CC1VO = Dep(), Dep(), Dep(), Dep()
    ohc_in = din("ohc", [128, 8])
    K.dma("sp", OHC[:], ohc_in[:, :], writes=[CD], more=True)
    K.dma("sp", IDX[:], ptab.rearrange("s j -> j s"), writes=[CD], more=True)
    for sg in range(8):
        ts(IDXALL[:, sg * 8:(sg + 1) * 8], IDX[:], 8.0, ALU.mult, [CD], [SMD], s2=float(sg), op1=ALU.add)
    ck2 = ck.rearrange("l n (g s) d -> (l n g) (s d)", g=8)
    cv2 = cv.rearrange("l n (g s) d -> (l n g) (s d)", g=8)
    IDXL1 = sb("IDXL1", [128, 64], I32)
    ts(IDXL1[:], IDXALL[:], float((8 if DEBUG_SMALL else NPOOL) * 8), ALU.add, [SMD], [SMD])
    IDXL = [IDXALL, IDXL1]
    SC = float(DH) ** -0.5
    PSB = [PS[i][:].bitcast(BF16) for i in range(8)]

    def cc(kind_groups, ins, outs, rd, wr):
        sem = es.enter_context(nc.semaphore("ccs%d" % cccount[0]))
        cccount[0] += 1
        e = K.engs["pool"]
        K._wait(e, K._events([rd], [wr]))
        nc.gpsimd.collective_compute("AllGather", ALU.bypass, replica_groups=kind_groups,
                                     ins=[ins.ap().opt()], outs=[outs.ap().opt()]).then_inc(sem)
        ev = (sem, 1)
        K.trace.setdefault("pool", []).append(("inc", id(sem), 1, "cc"))
        ccevs.append(ev)
        rd.r.append(ev)
        wr.w = [ev]
        wr.r = []

    PAIRS = [[0, 1], [2, 3], [4, 5], [6, 7]]

    def proj_tm(l, col0, hp, tc_fn):
        sv_, sdp = load_slab(w_in[l], 0, 16, col0 + hp * 256, 256)
        for tci, (t0, r) in enumerate(TCS):
            pi = tci % 2
            for kc in range(16):
                mm(PS[pi][:r, 0:256], XNv[:, kc, t0:t0 + r], sv_[:, kc, :], kc == 0, kc == 15, [sdp, XND], [PD[pi]])
            tc_fn(tci, t0, r, PS[pi], PD[pi])

    def rstd_small(r, n):
        act(SM2[:r, 0:n], SM[:r, 0:n], AF.Ln, [SMD], [SM2D], scale=1.0 / DH, bias=EPS)
        act(SM3[:r, 0:n], SM2[:r, 0:n], AF.Exp, [SM2D], [SM3D], scale=-0.5)

    def mixer_A(l):
        K.dma("sp", RETGN[:], retgn_in[l].partition_broadcast(128), writes=[CD])
        K.dma("sp", SBQ[:], sbq_in[l].partition_broadcast(128), writes=[CD], more=True)
        K.dma("sp", SBK[:], sbk_in[l].partition_broadcast(128), writes=[CD], more=True)
        K.dma("sp", SBB[:], sbb_in[l].partition_broadcast(128), writes=[CD], more=True)
        ts(SM[:, 0:1], FLAG[:, 0:1], -1.0, ALU.add, [CD], [SMD], s2=1.0e4, op1=ALU.mult)
        ts(SBBP[:], SBB[:], SM[:, 0:1], ALU.add, [CD, SMD], [SM2D])
        tt(SM2[:, 0:8], SBB[:], OHC[:], ALU.mult, [CD], [SM2D])
        K.op("dve", lambda e: e.tensor_reduce(out=BOWN[:], in_=SM2[:, 0:8], axis=AX.X, op=ALU.add), [SM2D], [SM3D])

        if stop_after == "mixA0":
            return
        for kind, col0 in (("k", 7168), ("v", 8192), ("q", 6144)):
            if kind not in os.environ.get("A1KINDS", "kvq"):
                continue
            for hp in range(4):
                def fn(tci, t0, r, P, Pd, kind=kind, hp=hp):
                    tb = 2 + (tci % 2)
                    A1M = int(os.environ.get("A1M", "9"))
                    if A1M == 0:
                        cp(T[tb][:r, 0:256], P[:r, 0:256], [Pd], [TD[tb]], en="act")
                        return
                    if kind == "v":
                        cp(T[tb][:r, 0:256], P[:r, 0:256], [Pd], [TD[tb]], en="act")
                        if "n" not in os.environ.get("VSKIP", ""):
                            K.dma("sp", nv_out[l, t0:t0 + r, hp * 256:(hp + 1) * 256], T[tb][:r, 0:256], reads=[TD[tb]])
                        if tci < 8 and "c" not in os.environ.get("VSKIP", ""):
                            cp(SQ[tci % 2][:r, 0:256], P[:r, 0:256], [Pd], [SQD[tci % 2]])
                            K.dma("sp", cc1v_in[l][hp * 256:(hp + 1) * 256, :].rearrange("(t a) b -> t (a b)", a=2)[t0:t0 + r, :].rearrange("t (a b) -> t a b", a=1)[:, 0, :] if False else cc1v_in[l].rearrange("(g t) c -> g (t c)", g=4)[hp, t0 * 256:(t0 + r) * 256].rearrange("(t c) -> t c", c=256), SQ[tci % 2][:r, 0:256],
                                  reads=[SQD[tci % 2]], writes=[CC1D], more=True)
                        return
                    G = SBK if kind == "k" else SBQ
                    act(T[0][:r, 0:256], P[:r, 0:256], AF.Square, [Pd], [TD[0]])
                    K.op("dve", lambda e: e.tensor_reduce(out=SM[:r, 0:2], in_=T[0][:r, 0:256].rearrange("p (h e) -> p h e", h=2),
                                                          axis=AX.X, op=ALU.add), [TD[0]], [SMD])
                    if A1M == 1:
                        return
                    rstd_small(r, 2)
                    for hh in range(2):
                        stt(T[tb][:r, hh * 128:(hh + 1) * 128], P[:r, hh * 128:(hh + 1) * 128], SM3[:r, hh:hh + 1], G[:r, :],
                            ALU.mult, ALU.mult, [Pd, SM3D, CD], [TD[tb]])
                    if A1M == 2:
                        return
                    if kind == "k":
                        K.dma("sp", nk_out[l, t0:t0 + r, hp * 256:(hp + 1) * 256], T[tb][:r, 0:256], reads=[TD[tb]])
                    elif tci == 8:
                        for hh in range(2):
                            h = hp * 2 + hh
                            if h == 0:
                                ts(QSEL[:r, :], T[tb][:r, 0:128], OHC[:r, 0:1], ALU.mult, [TD[tb], CD], [QSELD])
                            else:
                                stt(QSEL[:r, :], T[tb][:r, hh * 128:(hh + 1) * 128], OHC[:r, h:h + 1], QSEL[:r, :], ALU.mult, ALU.add,
                                    [TD[tb], CD], [QSELD])
                    if A1M == 3:
                        return
                    sq_ = SQ[tci % 2]
                    cp(sq_[:r, 0:256], T[tb][:r, 0:256], [TD[tb]], [SQD[tci % 2]], en="act")
                    pt = 4 + (tci % 2)
                    for hh in range(2):
                        tr(PSB[pt][:, hh * 128:hh * 128 + r], sq_[:r, hh * 128:(hh + 1) * 128], IDB[:r, :r], [SQD[tci % 2], CD], [PD[pt]])
                    cp(KT2[:, :, t0:t0 + r], PSB[pt][:, 0:256].rearrange("p (h t) -> p h t", h=2)[:, :, :r], [PD[pt]], [KT2D])
                proj_tm(l, col0, hp, fn)
                if kind == "k":
                    for hh in range(2):
                        h = hp * 2 + hh
                        K.dma("sp", cc1k_in[l][h * 128:(h + 1) * 128, :], KT2[:, hh, 0:1024], reads=[KT2D], writes=[CC1D], more=True)
                elif kind == "q":
                    for hh in range(2):
                        h = hp * 2 + hh
                        K.dma("sp", qt_scr[l, h * 128:(h + 1) * 128, :], KT2[:, hh, :], reads=[KT2D], writes=[QTD], more=True)
        K.dma("sp", qs_scr[l], QSEL[:8, :], reads=[QSELD], writes=[QSD])
        if stop_after == "mixA1":
            return

        for kind, col0 in (("k", 1024), ("v", 2048), ("q", 0)):
            for hp in range(4):
                def fn(tci, t0, r, P, Pd, kind=kind, hp=hp):
                    hs = slice(hp * 256, (hp + 1) * 256)
                    if kind == "v":
                        if tci == 8:
                            cp(SV[:r, hs], P[:r, 0:256], [Pd], [SVD], en="act")
                        else:
                            cp(SQ[tci % 2][:r, 0:256], P[:r, 0:256], [Pd], [SQD[tci % 2]], en="act")
                            K.dma("sp", rvs[l, tci, :r, hs], SQ[tci % 2][:r, 0:256], reads=[SQD[tci % 2]], writes=[RQD], more=True)
                        return
                    P4 = P[:r, 0:256].rearrange("p (a e) -> p a e", a=4)
                    cosb = COSv[:r, tci, :].unsqueeze(1).to_broadcast([r, 4, 64])
                    sinb = SINv[:r, tci, :].unsqueeze(1).to_broadcast([r, 4, 64])
                    tt(T[0][:r, 0:256].rearrange("p (a e) -> p a e", a=4), P4, cosb, ALU.mult, [Pd, CD], [TD[0]])
                    tt(T[1][:r, 0:256].rearrange("p (a e) -> p a e", a=4), P4, sinb, ALU.mult, [Pd, CD], [TD[1]])
                    A5 = T[0][:r, 0:256].rearrange("p (h f e) -> p h f e", h=2, f=2)
                    B5 = T[1][:r, 0:256].rearrange("p (h f e) -> p h f e", h=2, f=2)
                    R5 = T[2][:r, 0:256].rearrange("p (h f e) -> p h f e", h=2, f=2)
                    tt(R5[:, :, 0, :], A5[:, :, 0, :], B5[:, :, 1, :], ALU.subtract, [TD[0], TD[1]], [TD[2]])
                    tt(R5[:, :, 1, :], A5[:, :, 1, :], B5[:, :, 0, :], ALU.add, [TD[0], TD[1]], [TD[2]])
                    R3 = T[2][:r, 0:256].rearrange("p (h e) -> p h e", h=2)

                    def dec(k_):
                        return DECv[:r, k_, tci, hp * 2:hp * 2 + 2].unsqueeze(2).to_broadcast([r, 2, 128])
                    sq_ = SQ[tci % 2]
                    if kind == "q":
                        if tci == 8:
                            for hh in range(2):
                                tr(PS[6][:, hh * 8:hh * 8 + r], T[2][:r, hh * 128:(hh + 1) * 128], IDF[:r, :r], [TD[2], CD], [PD[6]])
                            cp(SQT[:, hp * 2:hp * 2 + 2, :], PS[6][:, 0:16].rearrange("p (h s) -> p h s", h=2), [PD[6]], [SQTD])
                        tt(sq_[:r, 0:256].rearrange("p (h e) -> p h e", h=2), R3, dec(DQ), ALU.mult, [TD[2], CD], [SQD[tci % 2]])
                    else:
                        if tci == 8:
                            tt(SK[:r, hs].rearrange("p (h e) -> p h e", h=2), R3, dec(DDEC), ALU.mult, [TD[2], CD], [SKD])
                            return
                        tt(T[3][:r, 0:256].bitcast(BF16)[:, 0:256].rearrange("p (h e) -> p h e", h=2), R3, dec(DDEC), ALU.mult,
                           [TD[2], CD], [TD[3]])
                        K.dma("sp", rkd[l, tci, :r, hs], T[3][:r, 0:256].bitcast(BF16)[:, 0:256], reads=[TD[3]], writes=[RQD], more=True)
                        tt(T[3][:r, 256:512].bitcast(BF16)[:, 0:256].rearrange("p (h e) -> p h e", h=2), R3, dec(DTOT), ALU.mult,
                           [TD[2], CD], [TD[3]])
                        K.dma("sp", rktot[l, tci, :r, hs], T[3][:r, 256:512].bitcast(BF16)[:, 0:256], reads=[TD[3]], writes=[RQD], more=True)
                        tt(sq_[:r, 0:256].rearrange("p (h e) -> p h e", h=2), R3, dec(DINV), ALU.mult, [TD[2], CD], [SQD[tci % 2]])
                    pt = 4 + (tci % 2)
                    for hh in range(2):
                        tr(PSB[pt][:, hh * 128:hh * 128 + r], sq_[:r, hh * 128:(hh + 1) * 128], IDB[:r, :r], [SQD[tci % 2], CD], [PD[pt]])
                    kb = (tci % 2)
                    cp(KT2[:, kb, 0:256].rearrange("p (h t) -> p h t", h=2)[:, :, :r],
                       PSB[pt][:, 0:256].rearrange("p (h t) -> p h t", h=2)[:, :, :r], [PD[pt]], [KT2D], en="act")
                    dst = rqt if kind == "q" else rkt
                    K.dma("sp", dst[l, tci, :, hs].rearrange("p (h t) -> p h t", h=2)[:, :, :r],
                          KT2[:, kb, 0:256].rearrange("p (h t) -> p h t", h=2)[:, :, :r], reads=[KT2D], writes=[RQD], more=True)
                proj_tm(l, col0, hp, fn)
        for tci in range(8):
            b0, b1 = RB[:, 0:1024], RB[:, 1024:2048]
            K.dma("sp", b0, rktot[l, tci], reads=[RQD], writes=[RBD[0]])
            K.dma("sp", b1, rvs[l, tci], reads=[RQD], writes=[RBD[1]])
            for h in range(8):
                mm(PS[h // 4][:, (h % 4) * 128:(h % 4 + 1) * 128], b0[:, h * 128:(h + 1) * 128], b1[:, h * 128:(h + 1) * 128],
                   True, True, [RBD[0], RBD[1]], [PD[h // 4]])
            for hf_ in range(2):
                if tci == 0:
                    cp(T[0][:, hf_ * 512:(hf_ + 1) * 512], PS[hf_][:, :], [PD[hf_]], [TD[0]])
                else:
                    tt(T[0][:, hf_ * 512:(hf_ + 1) * 512], T[0][:, hf_ * 512:(hf_ + 1) * 512], PS[hf_][:, :], ALU.add, [PD[hf_]], [TD[0]])
        K.dma("sp", cc2_in[l][0:1024, :].rearrange("(h d) e -> d h e", h=8), T[0][:, 0:1024].rearrange("p (h e) -> p h e", h=8),
              reads=[TD[0]], writes=[CC2D])
        if stop_after == "mixA2":
            return
        CHt = UBf[:, 0:16].rearrange("p (j c) -> p j c", j=2)
        CCt = UBf[:, 16:32].rearrange("p (j c) -> p j c", j=2)
        for which, col0, dstt in ((0, 3072, CHt), (1, 5120, CCt)):
            for sl in range(4):
                sv_, sdp = load_slab(w_in[l], 0, 16, col0 + sl * 256, 256)
                for o2 in range(2):
                    c = sl * 2 + o2
                    for kc in range(16):
                        mm(PS[6][:, 0:2], sv_[:, kc, o2 * 128:(o2 + 1) * 128], XNv[:, kc, 1022:1024], kc == 0, kc == 15, [sdp, XND], [PD[6]])
                    cp(dstt[:, :, c], PS[6][:, 0:2], [PD[6]], [UBD])
        UT = UBf[:, 32:48].rearrange("p (j c) -> p j c", j=2)
        tt(UT, CHt, CCt, ALU.mult, [UBD], [UBD])
        for j in range(2):
            K.dma("sp", cc2_in[l][1024 + j * 8:1032 + j * 8, :].rearrange("c p -> p c"), UT[:, j, :], reads=[UBD], writes=[CC2D], more=True)
            K.dma("sp", cs_out[l, j].rearrange("(c p) -> p c", c=8), UT[:, j, :], reads=[UBD])
        if stop_after == "mixA3":
            return
        cc(PAIRS, cc1k_in_t[l], cc1k_out_t[l], CC1D, CC1O)
        cc(PAIRS, cc1v_in_t[l], cc1v_out_t[l], CC1D, CC1VO)
        cc(PAIRS, cc2_in_t[l], cc2_out_t[l], CC2D, CC2O)

    def vview(ap2d, h):
        return ap2d.rearrange("(g t) c -> g (t c)", g=4)[h // 2].rearrange("(t c) -> t c", c=256)[
            :, (h % 2) * 128:(h % 2) * 128 + 128].rearrange("(c p) e -> p c e", p=128)

    def groupnorm_chunk(tci, t0, r, srcs=None, sdeps=None):
        if srcs is None:
            srcs = [PS[2], PS[3]]
            sdeps = [PD[2], PD[3]]
        for hf_ in range(2):
            act(T[0][:r, hf_ * 512:(hf_ + 1) * 512], srcs[hf_][:r, 0:512], AF.Square, [sdeps[hf_]], [TD[0]])
        K.op("dve", lambda e: e.tensor_reduce(out=SM[:r, 0:8], in_=T[0][:r, 0:1024].rearrange("p (h e) -> p h e", h=8),
                                              axis=AX.X, op=ALU.add), [TD[0]], [SMD])
        rstd_small(r, 8)
        yat = RB[:, 2048:3072]
        for h in range(8):
            stt(yat[:r, h * 128:(h + 1) * 128], srcs[h // 4][:r, (h % 4) * 128:(h % 4 + 1) * 128], SM3[:r, h:h + 1],
                RETGN[:r, h * 128:(h + 1) * 128], ALU.mult, ALU.mult, [sdeps[h // 4], SM3D, CD], [RBD[2]])
        for h in range(8):
            tr(PSB[6][:, h * 128:h * 128 + r], yat[:r, h * 128:(h + 1) * 128], IDB[:r, :r], [RBD[2], CD], [PD[6]])
        cp(YAv[:, :, t0:t0 + r], PSB[6][:, 0:1024].rearrange("p (h t) -> p h t", h=8)[:, :, :r], [PD[6]], [YAD])

    def mixer_B(l):
        AK = ATT[:, 0:2048]
        AVv = ATT[:, 2048:4096].rearrange("p (c e) -> p c e", c=16)
        AQ = ATT[:, 4096:5120]
        AKD, AVD, AQD = Dep("ak"), Dep("av"), Dep("aq")
        THD = [[Dep("th%d_%d" % (k_, p_)) for p_ in range(2)] for k_ in range(4)]
        for h in range(8):
            K.dma("sp", AK[:, 0:1024], cc1k_out[l][h * 128:(h + 1) * 128, :], reads=[CC1O], writes=[AKD])
            K.dma("sp", AK[:, 1024:2048], cc1k_in[l][h * 128:(h + 1) * 128, :], reads=[CC1D], writes=[AKD], more=True)
            K.dma("sp", AVv[:, 0:8, :], vview(cc1v_out[l][0:1024, :], h), reads=[CC1VO], writes=[AVD])
            K.dma("sp", AVv[:, 8:16, :], vview(cc1v_in[l], h), reads=[CC1D], writes=[AVD], more=True)
            K.dma("sp", AQ, qt_scr[l, h * 128:(h + 1) * 128, 0:1024], reads=[QTD], writes=[AQD])
            for Q in range(2):
                blocks = [(True, kb) for kb in range(4 * Q + 3, -1, -1)] + [(False, kb) for kb in range(7, -1, -1)]
                for bi, (own, kb) in enumerate(blocks):
                    ci = 8 + kb if own else kb
                    first, last = bi == 0, bi == len(blocks) - 1
                    par = bi % 2
                    hs_ = slice(par * 512, par * 512 + 512)
                    E_, SPf, EA_, R_ = T[0][:, hs_], T[1][:, hs_], T[2][:, hs_], T[3][:, hs_]
                    Ed, SPd, EAd, Rd = THD[0][par], THD[1][par], THD[2][par], THD[3][par]
                    spb, spbd = SQB[par], SQBD[par]
                    ab, abd = SQB[2 + par], SQBD[2 + par]
                    pz, pa, pb = par, 2 + 3 * par, 4 + 2 * par
                    mm(PS[pz][:, :512], AK[:, ci * 128:(ci + 1) * 128], AQ[:, Q * 512:(Q + 1) * 512], True, True, [AKD, AQD], [PD[pz]])
                    bias = (SBB if own else SBBP)[:, h:h + 1]
                    act(E_, PS[pz][:, :512], AF.Exp, [PD[pz], CD, SM2D], [Ed], scale=SC, bias=bias)
                    act(SPf, E_, AF.Ln, [Ed], [SPd], bias=1.0)
                    r_ = kb - 4 * Q
                    diag = own and r_ >= 0
                    if diag:
                        tt(spb[:, :512], SPf, MSB[:, r_ * 512:(r_ + 1) * 512], ALU.mult, [SPd, CD], [spbd])
                    else:
                        cp(spb[:, :512], SPf, [SPd], [spbd])
                    mm(PS[pa][:, :512], NU[:], spb[:, :512], True, True, [spbd, CD], [PD[pa]])
                    mm(PS[pb][:, :512], ONES[:], spb[:, :512], True, True, [spbd, CD], [PD[pb]])
                    act(R_, SPf, AF.Exp, [SPd], [Rd], scale=-1.0)
                    act(EA_, PS[pa][:, :512], AF.Exp, [PD[pa]], [EAd])
                    tt(R_, R_, E_, ALU.mult, [Ed], [Rd])
                    if diag:
                        tt(R_, R_, MSB[:, r_ * 512:(r_ + 1) * 512], ALU.mult, [CD], [Rd])
                    if not first:
                        tt(EA_, EA_, UB[:, 0:512], ALU.mult, [UBD], [EAd])
                    if not last:
                        if first:
                            act(UB[:, 0:512], PS[pb][:, :512], AF.Exp, [PD[pb]], [UBD], scale=-1.0)
                        else:
                            act(SPf, PS[pb][:, :512], AF.Exp, [PD[pb]], [SPd], scale=-1.0)
                            tt(UB[:, 0:512], UB[:, 0:512], SPf, ALU.mult, [SPd], [UBD], en="pool")
                    tt(ab[:, :512], R_, EA_, ALU.mult, [Rd, EAd], [abd])
                    mm(PS[3][:, :512], AVv[:, ci, :], ab[:, :512], first, last, [AVD, abd], [PD[3]])
                cp(YCv[:, h, Q * 512:(Q + 1) * 512], PS[3][:, :512], [PD[3]], [YCD], en="act")
        if stop_after == "mixB2":
            return
        barrier()
        KS = [ATTF[:, 0:2048], ATTF[:, 2048:4096]]
        KSD = [Dep("ks0"), Dep("ks1")]
        QBC = RB[:, 0:2048].bitcast(F32)
        K.dma("sp", QBC, qs_scr[l].rearrange("s d -> (s d)").partition_broadcast(128), reads=[QSD], writes=[RBD[0]])
        ZS = T[0][:, 0:1024].rearrange("p (s j) -> p s j", s=8)
        gi = 0
        for s in range(8):
            for sg in range(8):
                ks, ksd = KS[gi % 2], KSD[gi % 2]
                gi += 1
                K.dma("pool", None, None, reads=[CD, SMD], writes=[ksd],
                      fn=lambda e, ks=ks, sg=sg, s=s: e.indirect_dma_start(
                          out=ks, out_offset=None, in_=ck2,
                          in_offset=bass.IndirectOffsetOnAxis(ap=IDXL[l][:, sg * 8 + s:sg * 8 + s + 1], axis=0)))
                k3 = ks.rearrange("p (j d) -> p j d", j=16)
                tt(k3, k3, QBC[:, s * 128:(s + 1) * 128].unsqueeze(1).to_broadcast([128, 16, 128]), ALU.mult, [RBD[0]], [ksd])
                K.op("dve", lambda e, k3=k3, s=s, sg=sg: e.tensor_reduce(out=ZS[:, s, sg * 16:(sg + 1) * 16], in_=k3, axis=AX.X, op=ALU.add),
                     [ksd], [TD[0]])
        E_, SP_, A_, B_ = T[0][:, 0:1024], T[1][:, 0:1024], T[2][:, 0:1024], T[3][:, 0:1024]
        act(E_, E_, AF.Exp, [SM3D], [TD[0]], scale=SC, bias=BOWN[:, 0:1])
        act(SP_, E_, AF.Ln, [TD[0]], [TD[1]], bias=1.0)
        sp3 = SP_.rearrange("p (s j) -> p s j", s=8)
        K.op("dve", lambda e: e.tensor_reduce(out=SM[:, 0:8], in_=sp3, axis=AX.X, op=ALU.add), [TD[1]], [SMD])
        mm(PS[0][:, 0:8], USF[:], SM[:, 0:8], True, True, [SMD, CD], [PD[0]])
        cp(SM2[:, 0:8], PS[0][:, 0:8], [PD[0]], [SM2D])
        srcb, srcd = SP_, TD[1]
        for step, sh in enumerate((1, 2, 4, 8, 16, 32, 64)):
            dstb, dstd = (A_, TD[2]) if step % 2 == 0 else (B_, TD[3])
            s3 = srcb.rearrange("p (s j) -> p s j", s=8)
            d3 = dstb.rearrange("p (s j) -> p s j", s=8)
            tt(d3[:, :, 0:128 - sh], s3[:, :, 0:128 - sh], s3[:, :, sh:128], ALU.add, [srcd], [dstd])
            cp(d3[:, :, 128 - sh:128], s3[:, :, 128 - sh:128], [srcd], [dstd])
            srcb, srcd = dstb, dstd
        tt(B_, A_, SP_, ALU.subtract, [TD[2], TD[1]], [TD[3]])
        b3 = B_.rearrange("p (s j) -> p s j", s=8)
        tt(b3, b3, SM2[:, 0:8].unsqueeze(2).to_broadcast([128, 8, 128]), ALU.add, [SM2D], [TD[3]])
        act(A_, B_, AF.Exp, [TD[3]], [TD[2]], scale=-1.0)
        ts(B_, E_, 1.0, ALU.add, [TD[0]], [TD[3]])
        K.op("dve", lambda e: e.reciprocal(B_, B_), [], [TD[3]])
        tt(B_, B_, E_, ALU.mult, [TD[0]], [TD[3]])
        tt(B_, B_, A_, ALU.mult, [TD[2]], [TD[3]])
        AM = T[1][:, 0:1024].rearrange("p (j s) -> p j s", j=128)
        OHv = OH[:].rearrange("p (s t) -> p s t", s=8)
        for s in range(8):
            tt(AM, b3[:, s, :].unsqueeze(2).to_broadcast([128, 128, 8]), OHv[:, s, :].unsqueeze(1).to_broadcast([128, 128, 8]),
               ALU.mult, [TD[3], CD], [TD[1]])
            for sg in range(8):
                ks, ksd = KS[gi % 2], KSD[gi % 2]
                gi += 1
                K.dma("pool", None, None, reads=[CD, SMD], writes=[ksd],
                      fn=lambda e, ks=ks, sg=sg, s=s: e.indirect_dma_start(
                          out=ks, out_offset=None, in_=cv2,
                          in_offset=bass.IndirectOffsetOnAxis(ap=IDXL[l][:, sg * 8 + s:sg * 8 + s + 1], axis=0)))
                for j in range(16):
                    slot = sg * 16 + j
                    mm(PS[1][:8, 0:128], AM[:, slot, :], ks[:, j * 128:(j + 1) * 128], s == 0 and slot == 0, s == 7 and slot == 127,
                       [TD[1], ksd], [PD[1]])
        cp(SM2[:8, 0:128] if False else UBf[:8, 64:192], PS[1][:8, 0:128], [PD[1]], [UBD])
        K.dma("sp", cc3_in[l], UBf[:8, 64:192], reads=[UBD], writes=[CC3D])
        cc([list(range(8))], cc3_in_t[l], cc3_out_t[l], CC3D, CC3O)
        K.dma("sp", T[2][:64, 0:128], cc3_out[l], reads=[CC3O], writes=[TD[2]])
        tr(PS[0][:, 0:64], T[2][:64, 0:128], IDF[:64, :64], [TD[2], CD], [PD[0]])
        cp(YCv[:, :, 1024:1032], PS[0][:, 0:64].rearrange("p (h s) -> p h s", h=8), [PD[0]], [YCD])
        if stop_after == "mixB3":
            return
        barrier()
        SF = T[3][:, 0:1024].rearrange("p (h e) -> p h e", h=8)
        SBF = RB[:, 3072:4096].rearrange("p (h e) -> p h e", h=8)
        K.dma("sp", SF, cc2_out[l][0:1024, :].rearrange("(h d) e -> d h e", h=8), reads=[CC2O], writes=[TD[3]])
        ts(T[3][:, 0:1024], T[3][:, 0:1024], FLAG[:, 0:1], ALU.mult, [CD], [TD[3]])
        cp(RB[:, 3072:4096], T[3][:, 0:1024], [TD[3]], [RBD[3]], en="act")
        QTb, KTb = ATT[:, 0:1024], ATT[:, 1024:2048]
        KDb, VVb = ATT[:, 2048:3072], ATT[:, 3072:4096]
        LD = [Dep("qtb"), Dep("ktb"), Dep("kdb"), Dep("vvb")]
        for tci in range(8):
            t0 = tci * 128
            K.dma("sp", QTb, rqt[l, tci], reads=[RQD], writes=[LD[0]])
            K.dma("sp", KTb, rkt[l, tci], reads=[RQD], writes=[LD[1]])
            K.dma("sp", KDb, rkd[l, tci], reads=[RQD], writes=[LD[2]])
            K.dma("sp", VVb, rvs[l, tci], reads=[RQD], writes=[LD[3]])
            for h in range(8):
                hs = slice(h * 128, (h + 1) * 128)
                pz = h % 2
                mm(PS[pz][:, 0:128], KTb[:, hs], QTb[:, hs], True, True, [LD[0], LD[1]], [PD[pz]])
                tt(SQ[pz][:, 0:128], PS[pz][:, 0:128], M01[:], ALU.mult, [PD[pz], CD], [SQD[pz]])
                po = PS[2 + h // 4][:, (h % 4) * 128:(h % 4 + 1) * 128]
                mm(po, SQ[pz][:, 0:128], VVb[:, hs], True, False, [SQD[pz], LD[3]], [PD[2 + h // 4]])
                mm(po, QTb[:, hs], SBF[:, h, :], False, True, [LD[0], RBD[3]], [PD[2 + h // 4]])
                pd_ = 4 + (h % 2)
                mm(PS[pd_][:, 0:128], KDb[:, hs], VVb[:, hs], True, True, [LD[2], LD[3]], [PD[pd_]])
                stt(SF[:, h, :], SF[:, h, :], float(GAMMA[h] ** 128), PS[pd_][:, 0:128], ALU.mult, ALU.add, [PD[pd_]], [TD[3]])
                cp(SBF[:, h, :], SF[:, h, :], [TD[3]], [RBD[3]], en="act")
            groupnorm_chunk(tci, t0, 128)
        K.dma("sp", rs_out[l].rearrange("h d e -> d h e"), SF, reads=[TD[3]])
        SS = T[0][:, 0:1024].rearrange("p (h e) -> p h e", h=8)
        OHv = OH[:].rearrange("p (s t) -> p s t", s=8)
        for s in range(8):
            K.dma("sp", SS, st_ret[l, s].rearrange("h d e -> d h e"), writes=[TD[0]])
            ts(T[1][:8, 0:1024], SV[:8, :], IDF[:8, s:s + 1], ALU.mult, [SVD, CD], [TD[1]])
            QM = SM[:, 0:64].rearrange("p (h t) -> p h t", h=8)
            tt(QM, SQT, OHv[:, s, :].unsqueeze(1).to_broadcast([128, 8, 8]), ALU.mult, [SQTD, CD], [SMD])
            for h in range(8):
                hs = slice(h * 128, (h + 1) * 128)
                pd_ = 4 + (h % 2)
                mm(PS[pd_][:, 0:128], SK[:8, hs], T[1][:8, hs], True, True, [SKD, TD[1]], [PD[pd_]])
                stt(SS[:, h, :], SS[:, h, :], float(GAMMA[h]), PS[pd_][:, 0:128], ALU.mult, ALU.add, [PD[pd_]], [TD[0]])
            K.dma("sp", rss_out[l, s].rearrange("h d e -> d h e"), SS, reads=[TD[0]])
            for h in range(8):
                po = PS[2 + h // 4][:8, (h % 4) * 128:(h % 4 + 1) * 128]
                mm(po, QM[:, h, :], SS[:, h, :], True, True, [SMD, TD[0]], [PD[2 + h // 4]])
            for hf_ in range(2):
                osl = T[2][:8, hf_ * 512:(hf_ + 1) * 512]
                if s == 0:
                    cp(osl, PS[2 + hf_][:8, :], [PD[2 + hf_]], [TD[2]])
                else:
                    tt(osl, osl, PS[2 + hf_][:8, :], ALU.add, [PD[2 + hf_]], [TD[2]])
        groupnorm_chunk(8, 1024, 8, [T[2][:, 0:512], T[2][:, 512:1024]], [TD[2], TD[2]])
    def mixer_C(l):
        HALO = UBf[:, 48:64].rearrange("p (j c) -> p j c", j=2)
        for j in range(2):
            K.dma("sp", HALO[:, j, :], cc2_out[l][1024 + j * 8:1032 + j * 8, :].rearrange("c p -> p c"), reads=[CC2O], writes=[UBD],
                  more=(j == 1))
        ts(UBf[:, 48:64], UBf[:, 48:64], FLAG[:, 0:1], ALU.mult, [CD], [UBD])
        SPREV = UBf[:, 192:320].rearrange("p (c t) -> p c t", c=8)
        CSS = UBf[:, 320:448].rearrange("p (c t) -> p c t", c=8)
        K.dma("sp", T[0][:16, 0:1024], st_conv[l], writes=[TD[0]])
        for c in range(8):
            tr(PS[6][:, c * 16:(c + 1) * 16], T[0][:16, c * 128:(c + 1) * 128], IDF[:16, :16], [TD[0], CD], [PD[6]])
        cp(SPREV, PS[6][:, 0:128].rearrange("p (c t) -> p c t", c=8), [PD[6]], [UBD])
        for c in range(8):
            sa, sad = load_slab(w_in[l], 0, 16, 3072 + c * 128, 128)
            load_slab(w_in[l], 0, 16, 5120 + c * 128, 128, off=2048, more=True)
            sa2 = WS[slab_i[0]][:, 2048:4096].rearrange("p (k n) -> p k n", k=16)
            sbv, sbd = load_slab(w_in[l], 0, 16, 4096 + c * 128, 128)
            for (c0, n) in CTS:
                for kc in range(16):
                    mm(PS[0][:, :n], sa[:, kc, :], XNv[:, kc, c0:c0 + n], kc == 0, kc == 15, [sad, XND], [PD[0]])
                for kc in range(16):
                    mm(PS[1][:, :n], sa2[:, kc, :], XNv[:, kc, c0:c0 + n], kc == 0, kc == 15, [sad, XND], [PD[1]])
                for kc in range(16):
                    mm(PS[2][:, :n], sbv[:, kc, :], XNv[:, kc, c0:c0 + n], kc == 0, kc == 15, [sbd, XND], [PD[2]])
                cp(T[0][:, :n], PS[0][:, :n], [PD[0]], [TD[0]], en="act")
                tt(UB[:, 2 + c0:2 + c0 + n], T[0][:, :n], PS[1][:, :n], ALU.mult, [TD[0], PD[1]], [UBD])
                cp(T[1][:, c0:c0 + n], PS[2][:, :n], [PD[2]], [TD[1]], en="act")
            cp(UB[:, 0:2], HALO[:, :, c], [UBD], [UBD])
            w0, w1, w2 = [CONVW[:, l * 24 + c * 3 + j:l * 24 + c * 3 + j + 1] for j in range(3)]
            ts(T[2][:, 0:1024], UB[:, 2:1026], w2, ALU.mult, [UBD, CD], [TD[2]])
            stt(T[2][:, 0:1024], UB[:, 1:1025], w1, T[2][:, 0:1024], ALU.mult, ALU.add, [UBD, CD], [TD[2]])
            stt(T[2][:, 0:1024], UB[:, 0:1024], w0, T[2][:, 0:1024], ALU.mult, ALU.add, [UBD, CD], [TD[2]])
            sp3 = SPREV[:, c, :].rearrange("p (s j) -> p s j", s=8)
            ts(T[2][:, 1024:1032], UB[:, 1026:1034], w2, ALU.mult, [UBD, CD], [TD[2]])
            stt(T[2][:, 1024:1032], sp3[:, :, 1], w1, T[2][:, 1024:1032], ALU.mult, ALU.add, [UBD, CD], [TD[2]])
            stt(T[2][:, 1024:1032], sp3[:, :, 0], w0, T[2][:, 1024:1032], ALU.mult, ALU.add, [UBD, CD], [TD[2]])
            tt(YBv[:, c, :], T[1][:, 0:NT], T[2][:, 0:NT], ALU.mult, [TD[1], TD[2]], [YBD])
            cs3 = CSS[:, c, :].rearrange("p (s j) -> p s j", s=8)
            cp(cs3[:, :, 0], sp3[:, :, 1], [UBD], [UBD])
            cp(cs3[:, :, 1], UB[:, 1026:1034], [UBD], [UBD])
        for c in range(8):
            pb = 6 + c // 4
            tr(PS[pb][:16, (c % 4) * 128:(c % 4 + 1) * 128], CSS[:, c, :], IDF[:, :], [UBD, CD], [PD[pb]])
        for hf_ in range(2):
            cp(T[0][:16, hf_ * 512:(hf_ + 1) * 512], PS[6 + hf_][:16, :], [PD[6 + hf_]], [TD[0]])
        K.dma("sp", css_out[l], T[0][:16, 0:1024], reads=[TD[0]])
        if stop_after == "mixC1":
            return
        MGB = BIG[:, 2064:2064 + 2 * NT].rearrange("p (o t) -> p o t", o=2)
        MGBD = Dep("mgb")
        Ys = [(YAv, YAD), (YBv, YBD), (YCv, YCD)]
        for og in range(8):
            for br in range(3):
                gs, gsd = load_slab(w_in[l], 0, 16, 9216 + br * 2048 + og * 256, 256)
                bs, bsd = load_slab(w_br[br][l], 0, 8, og * 256, 256)
                Yv, Yd = Ys[br]
                for o2 in range(2):
                    for (c0, n) in CTS:
                        for kc in range(16):
                            mm(PS[0][:, :n], gs[:, kc, o2 * 128:(o2 + 1) * 128], XNv[:, kc, c0:c0 + n], kc == 0, kc == 15, [gsd, XND], [PD[0]])
                        for kc in range(8):
                            mm(PS[1][:, :n], bs[:, kc, o2 * 128:(o2 + 1) * 128], Yv[:, kc, c0:c0 + n], kc == 0, kc == 7, [bsd, Yd], [PD[1]])
                        act(T[0][:, :n], PS[0][:, :n], AF.Sigmoid, [PD[0]], [TD[0]])
                        M_ = T[2 + o2][:, c0:c0 + n]
                        Md = TD[2 + o2]
                        if br == 0:
                            tt(M_, T[0][:, :n], PS[1][:, :n], ALU.mult, [TD[0], PD[1]], [Md])
                        else:
                            tt(T[1][:, :n], T[0][:, :n], PS[1][:, :n], ALU.mult, [TD[0], PD[1]], [TD[1]])
                            if br == 1:
                                tt(M_, M_, T[1][:, :n], ALU.add, [TD[1]], [Md])
                            else:
                                tt(MGB[:, o2, c0:c0 + n], M_, T[1][:, :n], ALU.add, [TD[1], Md], [MGBD])
            for o2 in range(2):
                K.dma("sp", mg_scr[l, og * 2 + o2], MGB[:, o2, :], reads=[MGBD], writes=[MGD], more=True)

    def outproj_ple(l):
        K.dma("sp", XNv, mg_scr[l].rearrange("c p t -> p c t"), reads=[MGD], writes=[XND])
        it = 0
        for og in range(8):
            sv_, sdp = load_slab(w_out[l], 0, 16, og * 256, 256)
            for o2 in range(2):
                oc = og * 2 + o2
                for (c0, n) in CTS:
                    pz = it % 2
                    it += 1
                    for kc in range(16):
                        mm(PS[pz][:, :n], sv_[:, kc, o2 * 128:(o2 + 1) * 128], XNv[:, kc, c0:c0 + n], kc == 0, kc == 15, [sdp, XND], [PD[pz]])
                    tt(Xv[:, oc, c0:c0 + n], Xv[:, oc, c0:c0 + n], PS[pz][:, :n], ALU.add, [PD[pz]], [XD[oc]])

    def ple(l):
        rmsnorm(l * 4 + 3)
        PT = BIG[:, 0:2 * NT].rearrange("p (k t) -> p k t", k=2)
        for ti, (t0, r) in enumerate(TCS):
            K.dma("sp", T[ti % 2][:r, 0:256], p_in[l, t0:t0 + r, :], writes=[TD[ti % 2]])
            for kc in range(2):
                tr(PS[6][:, kc * 128:kc * 128 + r], T[ti % 2][:r, kc * 128:(kc + 1) * 128], IDF[:r, :r], [TD[ti % 2], CD], [PD[6]])
            cp(PT[:, :, t0:t0 + r], PS[6][:, 0:256].rearrange("p (k t) -> p k t", k=2)[:, :, :r], [PD[6]], [BIGD])
        for og in range(8):
            gs, gsd = load_slab(w_pg[l], 0, 16, og * 256, 256)
            us, usd = load_slab(w_pu[l], 0, 2, og * 256, 256)
            for o2 in range(2):
                oc = og * 2 + o2
                for (c0, n) in CTS:
                    for kc in range(16):
                        mm(PS[0][:, :n], gs[:, kc, o2 * 128:(o2 + 1) * 128], XNv[:, kc, c0:c0 + n], kc == 0, kc == 15, [gsd, XND], [PD[0]])
                    for kc in range(2):
                        mm(PS[1][:, :n], us[:, kc, o2 * 128:(o2 + 1) * 128], PT[:, kc, c0:c0 + n], kc == 0, kc == 1, [usd, BIGD], [PD[1]])
                    act(T[2][:, :n], PS[0][:, :n], AF.Sigmoid, [PD[0]], [TD[2]])
                    tt(T[3][:, :n], T[2][:, :n], PS[1][:, :n], ALU.mult, [TD[2], PD[1]], [TD[3]])
                    tt(Xv[:, oc, c0:c0 + n], Xv[:, oc, c0:c0 + n], T[3][:, :n], ALU.add, [TD[3]], [XD[oc]])

    cc1k_in_t = [dscr("cc1ki%d" % l, [1024, 512], F32) for l in range(L)]
    cc1k_out_t = [dscr("cc1ko%d" % l, [2048, 512], F32) for l in range(L)]
    cc1v_in_t = [dscr("cc1vi%d" % l, [1024, 1024], BF16) for l in range(L)]
    cc1v_out_t = [dscr("cc1vo%d" % l, [2048, 1024], BF16) for l in range(L)]
    cc2_in_t = [dscr("cc2i%d" % l, [1040, 128], F32) for l in range(L)]
    cc2_out_t = [dscr("cc2o%d" % l, [2080, 128], F32) for l in range(L)]
    cc3_in_t = [dscr("cc3i%d" % l, [NS, 128], F32) for l in range(L)]
    cc3_out_t = [dscr("cc3o%d" % l, [8 * NS, 128], F32) for l in range(L)]
    cc1k_in = [t.ap().bitcast(BF16) for t in cc1k_in_t]
    cc1k_out = [t.ap().bitcast(BF16) for t in cc1k_out_t]
    cc1v_in = [t.ap() for t in cc1v_in_t]
    cc1v_out = [t.ap() for t in cc1v_out_t]
    cc2_in = [t.ap() for t in cc2_in_t]
    cc2_out = [t.ap() for t in cc2_out_t]
    cc3_in = [t.ap() for t in cc3_in_t]
    cc3_out = [t.ap() for t in cc3_out_t]
    qt_scr, rqt, rkt, rkd, rktot, rvs, mg_scr, qs_scr = [t.ap() for t in (qt_scr, rqt, rkt, rkd, rktot, rvs, mg_scr, qs_scr)]
    x_scr_ap = x_scr.ap()
    for l in range(L):
        if stop_after == "load":
            break
        rmsnorm(l * 4 + 0)
        ffn(l, 0)
        if stop_after == "ffn1":
            break
        if not (ENABLE_MIXER or stop_after):
            rmsnorm(l * 4 + 2)
            ffn(l, 1)
            continue
        rmsnorm(l * 4 + 1)
        K.dma("sp", x_scr_ap[:, :], X[:], reads=XD, writes=[XSD])
        barrier()
        mixer_A(l)
        barrier()
        if stop_after not in ("mixA", "mixA0", "mixA1", "mixA2", "mixA3"):
            mixer_B(l)
            barrier()
        if stop_after not in ("mixA", "mixA0", "mixA1", "mixA2", "mixA3", "mixB2", "mixB3", "mixB1"):
            mixer_C(l)
            barrier()
        if stop_after in ("mixA", "mixA0", "mixA1", "mixA2", "mixA3", "mixB2", "mixB3", "mixB1", "mixC1"):
            K.dma("sp", X[:], x_scr_ap[:, :], reads=[XSD], writes=XD)
            barrier()
            break
        K.dma("sp", X[:], x_scr_ap[:, :], reads=[XSD], writes=XD)
        barrier()
        outproj_ple(l)
        rmsnorm(l * 4 + 2)
        ffn(l, 1)
        ple(l)

    barrier()
    for ti, (t0, r) in enumerate(TCS):
        stg = BIGF[:, (ti % 2) * 2048:(ti % 2) * 2048 + 2048]
        sd = SQD[ti % 2]
        for g in range(4):
            pb = (ti * 4 + g) % 4
            for j in range(4):
                fc = g * 4 + j
                tr(PS[pb][:r, j * 128:(j + 1) * 128], Xv[:, fc, t0:t0 + r], IDF[:, :], [XD[fc], CD], [PD[pb]])
            cp(stg[:r, g * 512:(g + 1) * 512], PS[pb][:r, :], [PD[pb]], [sd], en=("act" if g % 2 else "dve"))
        K.dma("sp", y_out[t0:t0 + r, :], stg[:r, :], reads=[sd])
    barrier()
    nc._ktrace = K.trace
    return nc


def simulate(trace):
    sems = {}
    pcs = {k: 0 for k in trace}
    progress = True
    while progress:
        progress = False
        for k, tr_ in trace.items():
            while pcs[k] < len(tr_):
                kind, sid, val, tag = tr_[pcs[k]]
                if kind == "wait":
                    if sems.get(sid, 0) >= val:
                        pcs[k] += 1
                        progress = True
                    else:
                        break
                else:
                    sems[sid] = sems.get(sid, 0) + val
                    pcs[k] += 1
                    progress = True
    for k, tr_ in trace.items():
        if pcs[k] < len(tr_):
            print("BLOCKED", k, "at", pcs[k], "/", len(tr_), tr_[pcs[k]], "cur", sems.get(tr_[pcs[k]][1], 0))
    return all(pcs[k] == len(trace[k]) for k in trace)


def _bf16(a):
    return np.asarray(a, np.float32).astype(ml_dtypes.bfloat16)


def make_consts(hf):
    c = {}
    c["c_idf"] = np.eye(128, dtype=np.float32)
    c["c_idb"] = _bf16(np.eye(128))
    c["c_ones"] = _bf16(np.ones((128, 128)))
    j = np.arange(128)[:, None]
    i = np.arange(128)[None, :]
    c["c_m01"] = (i >= j).astype(np.float32)
    q = np.arange(512)[None, :]
    c["c_msb"] = np.concatenate([((r * 128 + j) < q).astype(np.float32) for r in range(4)], axis=1)
    c["c_nu"] = _bf16(-(j > i).astype(np.float32))
    c["c_nl"] = _bf16(-(j <= i).astype(np.float32))
    c["c_usf"] = (j > i).astype(np.float32)
    half = 64
    inv = (10000.0 ** (-np.arange(half, dtype=np.float32) / half)).astype(np.float32)
    pos = np.zeros((128, 9), np.float32)
    for tc in range(8):
        pos[:, tc] = hf * 1024 + tc * 128 + np.arange(128)
    pos[:, 8] = PAST
    ang = (pos[:, :, None] * inv[None, None, :]).astype(np.float32)
    c["c_cos"] = np.cos(ang).astype(np.float32).reshape(128, 9 * 64)
    c["c_sin"] = np.sin(ang).astype(np.float32).reshape(128, 9 * 64)
    lg = np.log1p(-np.exp2(-5.0 - np.arange(H, dtype=np.float64)))
    p = np.arange(128, dtype=np.float64)[:, None, None]
    tcs = np.arange(9, dtype=np.float64)[None, :, None]
    lgh = lg[None, None, :]
    sc = 128.0 ** -0.5
    dinv = np.exp(-lgh * (p + 1.0)) * sc * np.ones_like(tcs)
    ddec = np.exp(lgh * (127.0 - p)) * sc * np.ones_like(tcs)
    dtot = np.exp(lgh * (1023.0 - (tcs * 128 + p))) * sc
    dq = np.exp(lgh * (p + 1.0)) * np.ones_like(tcs)
    dinv[:, 8, :] = sc
    ddec[:, 8, :] = sc
    dtot[:, 8, :] = sc
    dq[:, 8, :] = 1.0
    c["c_dec"] = np.stack([dinv, ddec, dtot, dq], axis=1).astype(np.float32).reshape(128, 4 * 72)
    oh = np.zeros((128, 8, 8), np.float32)
    for s in range(8):
        oh[:, s, s] = 1.0
    c["c_oh"] = oh.reshape(128, 64)
    return c


_NC_CACHE = {}


def kernel(x_prompt, x_sample, cache_sb_k, cache_sb_v, state_ret, state_conv, page_table, p_prompt, p_sample,
           ffn1_norm, ffn1_w_gu, ffn1_w_down, mix_norm, w_in, ret_gn, conv_w, sb_q_norm, sb_k_norm, sb_bias,
           w_branch_ret, w_branch_conv, w_branch_sb, w_out, ffn2_norm, ffn2_w_gu, ffn2_w_down,
           ple_norm, w_ple_gate, w_ple_up):
    f = lambda a: np.ascontiguousarray(np.asarray(a))
    x_prompt, x_sample = f(x_prompt), f(x_sample)
    if "nc" not in _NC_CACHE:
        _NC_CACHE["nc"] = build(STOP_AFTER)
    nc = _NC_CACHE["nc"]
    gam = np.stack([f(ffn1_norm), f(mix_norm), f(ffn2_norm), f(ple_norm)], axis=1)
    gam = np.ascontiguousarray(gam.reshape(L * 4, 16, 128).transpose(2, 0, 1).reshape(128, L * 64))
    convw = np.ascontiguousarray(f(conv_w).reshape(L, 3, 8, 128).transpose(3, 0, 2, 1).reshape(128, L * 24))
    shared = {
        "ffn1_w_gu": f(ffn1_w_gu), "ffn2_w_gu": f(ffn2_w_gu), "ffn1_w_down": f(ffn1_w_down), "ffn2_w_down": f(ffn2_w_down),
        "w_in": f(w_in), "w_branch_ret": f(w_branch_ret), "w_branch_conv": f(w_branch_conv), "w_branch_sb": f(w_branch_sb),
        "w_out": f(w_out), "w_ple_gate": f(w_ple_gate), "w_ple_up": f(w_ple_up), "gam": gam,
        "ret_gn": f(ret_gn).reshape(L, 1024), "sb_q_norm": f(sb_q_norm), "sb_k_norm": f(sb_k_norm), "sb_bias": f(sb_bias),
        "convw": convw, "st_ret": f(state_ret), "st_conv": f(state_conv).reshape(L, NS * 2, 1024),
        "ptab": f(page_table).astype(np.int32),
    }
    ck_all, cv_all = np.asarray(cache_sb_k), np.asarray(cache_sb_v)
    psamp = f(p_sample).reshape(L, NS, 256)
    in_maps = []
    for c in range(8):
        b, hf = c // 2, c % 2
        m = dict(shared)
        m["x_in"] = np.concatenate([x_prompt[b, hf * NP:(hf + 1) * NP], x_sample[:, 0, :]], axis=0)
        m["p_in"] = np.concatenate([f(p_prompt)[:, b, hf * NP:(hf + 1) * NP], psamp], axis=1)
        m["ck"] = np.ascontiguousarray(ck_all[:, :, :, c, :])
        m["cv"] = np.ascontiguousarray(cv_all[:, :, :, c, :])
        m["flag"] = np.full((128, 1), float(hf), np.float32)
        ohc = np.zeros((128, 8), np.float32)
        ohc[:, c] = 1.0
        m["ohc"] = ohc
        m.update(make_consts(hf))
        in_maps.append(m)
    if DEBUG_SMALL:
        for m in in_maps:
            for k_ in ("ffn1_w_gu", "ffn2_w_gu", "ffn1_w_down", "ffn2_w_down"):
                m[k_] = np.zeros((L, 1, 1), np.float32)
            m["ck"] = m["ck"][:, :8]
            m["cv"] = m["cv"][:, :8]
            if DEBUG_SMALL == 2:
                for k_ in ("w_branch_ret", "w_branch_conv", "w_branch_sb", "w_out", "w_ple_gate"):
                    m[k_] = np.zeros((L, 1, 1), np.float32)
    res = run_bass_kernel_spmd(nc, in_maps, core_ids=list(range(8))).results
    B, S = 4, 2048
    yp = np.zeros((B, S, D), np.float32)
    nkp = np.zeros((L, B, S, H, DH), np.float32)
    nvp = np.zeros((L, B, S, H, DH), np.float32)
    rsp = np.zeros((L, B, H, DH, DH), np.float32)
    csp = np.zeros((L, B, 2, 1024), np.float32)
    for c in range(8):
        b, hf = c // 2, c % 2
        r = res[c]
        yp[b, hf * NP:(hf + 1) * NP] = r["y"][:NP]
        nkp[:, b, hf * NP:(hf + 1) * NP] = r["nk"][:, :NP].reshape(L, NP, H, DH)
        nvp[:, b, hf * NP:(hf + 1) * NP] = r["nv"][:, :NP].reshape(L, NP, H, DH)
        if hf == 1:
            rsp[:, b] = r["rs"]
            csp[:, b] = r["cs"]
    r0 = res[0]
    ys = r0["y"][NP:].reshape(NS, 1, D).copy()
    nks = r0["nk"][:, NP:].reshape(L, NS, 1, H, DH).copy()
    nvs = r0["nv"][:, NP:].reshape(L, NS, 1, H, DH).copy()
    rss = r0["rss"].copy()
    css = r0["css"].reshape(L, NS, 2, 1024).copy()
    return (yp, ys, nkp, nvp, nks, nvs, rsp, rss, csp, css)
```

```python
import contextlib
import os
import numpy as np
import ml_dtypes
import concourse.bass as bass
import concourse.mybir as mybir
from concourse.bass_utils import run_bass_kernel_spmd

F32 = mybir.dt.float32
BF16 = mybir.dt.bfloat16
I32 = mybir.dt.int32
AF = mybir.ActivationFunctionType
ALU = mybir.AluOpType
AX = mybir.AxisListType

D = 2048
L = 2
NP = 1024
NS = 8
NT = NP + NS
DFF = 5632
NIN = 15360
H = 8
DH = 128
PAST = 16384
NPAGE = 128
NPOOL = 1280
EPS = 1e-6
CTS = [(0, 512), (512, 512), (1024, 8)]
TCS = [(i * 128, 128) for i in range(8)] + [(1024, 8)]
GAMMA = [1.0 - 2.0 ** (-5.0 - h) for h in range(H)]
STOP_AFTER = None
DEBUG_SMALL = False
ENABLE_MIXER = True


class Dep:
    __slots__ = ("w", "r", "name", "excl")

    def __init__(self, name="", excl=False):
        self.w = []
        self.r = []
        self.name = name
        self.excl = excl


class Ctx:
    def __init__(self, nc, es):
        self.nc = nc
        self.es = es
        self.engs = {}
        self.dsems = []
        self.dnext = 0
        self.trace = {}
        self.tag = ""

    def add_engine(self, name, handle):
        sem = self.es.enter_context(self.nc.semaphore("sem_" + name))
        self.engs[name] = {"h": handle, "sem": sem, "count": 0, "wm": {}, "name": name}

    def add_dma_sems(self, n):
        self.rings = {"sp": [], "pool": []}
        self.rnext = {"sp": 0, "pool": 0}
        for i in range(n):
            sem = self.es.enter_context(self.nc.semaphore("dsem%d" % i))
            slot = [sem, 0]
            self.dsems.append(slot)
            self.rings["sp" if i < (2 * n) // 3 else "pool"].append(slot)

    def _wait(self, e, events):
        for (sem, val) in events:
            if e["name"] == "pe" and sem is e["sem"]:
                continue
            key = id(sem)
            if e["wm"].get(key, 0) < val:
                e["h"].wait_ge(sem, val)
                e["wm"][key] = val
                self.trace.setdefault(e["name"], []).append(("wait", key, val, self.tag))

    @staticmethod
    def _events(reads, writes):
        ev = []
        for d in reads:
            ev.extend(d.w)
            if d.excl:
                ev.extend(d.r)
        for d in writes:
            ev.extend(d.w)
            ev.extend(d.r)
        return ev

    def op(self, en, fn, reads=(), writes=(), inc=True):
        e = self.engs[en]
        self._wait(e, self._events(reads, writes))
        val = e["count"] + 1
        inst = fn(e["h"])
        if inc:
            inst.then_inc(e["sem"], 1)
            e["count"] = val
            self.trace.setdefault(e["name"], []).append(("inc", id(e["sem"]), 1, self.tag))
        ev = (e["sem"], val)
        for d in reads:
            d.r.append(ev)
        for d in writes:
            d.w = [ev]
            d.r = []
        return inst

    def dma(self, qn, out, in_, reads=(), writes=(), more=False, fn=None):
        e = self.engs[qn]
        if more:
            ev = []
            for d in reads:
                ev.extend(d.w)
        else:
            ev = self._events(reads, writes)
        self._wait(e, ev)
        ring = self.rings[qn]
        slot = ring[self.rnext[qn]]
        self.rnext[qn] = (self.rnext[qn] + 1) % len(ring)
        self._wait(e, [(slot[0], slot[1])])
        if fn is None:
            inst = e["h"].dma_start(out=out, in_=in_)
        else:
            inst = fn(e["h"])
        slot[1] += 16
        inst.then_inc(slot[0], 16)
        self.trace.setdefault(e["name"], []).append(("inc", id(slot[0]), 16, self.tag))
        evn = (slot[0], slot[1])
        for d in reads:
            d.r.append(evn)
        for d in writes:
            if more:
                d.w.append(evn)
            else:
                d.w = [evn]
                d.r = []
        return inst


def build(stop_after=None):
    nc = bass.Bass("TRN2", target_bir_lowering=False)
    es = contextlib.ExitStack()

    def din(name, shape, dt=F32):
        return nc.dram_tensor(name, list(shape), dt, kind="ExternalInput").ap()

    def dout(name, shape, dt=F32):
        return nc.dram_tensor(name, list(shape), dt, kind="ExternalOutput").ap()

    def dscr(name, shape, dt):
        return nc.dram_tensor(name, list(shape), dt)

    x_in = din("x_in", [NT, D])
    p_in = din("p_in", [L, NT, 256])
    ck = din("ck", [L, 8 if DEBUG_SMALL else NPOOL, 128, DH])
    cv = din("cv", [L, 8 if DEBUG_SMALL else NPOOL, 128, DH])
    st_ret = din("st_ret", [L, NS, H, DH, DH])
    st_conv = din("st_conv", [L, NS * 2, 1024])
    ptab = din("ptab", [NS, NPAGE], I32)
    if DEBUG_SMALL:
        w_gu = [din("ffn1_w_gu", [L, 1, 1]), din("ffn2_w_gu", [L, 1, 1])]
        w_dn = [din("ffn1_w_down", [L, 1, 1]), din("ffn2_w_down", [L, 1, 1])]
    else:
        w_gu = [din("ffn1_w_gu", [L, D, 2 * DFF]), din("ffn2_w_gu", [L, D, 2 * DFF])]
        w_dn = [din("ffn1_w_down", [L, DFF, D]), din("ffn2_w_down", [L, DFF, D])]
    w_in = din("w_in", [L, D, NIN])
    SMALLW = DEBUG_SMALL == 2
    w_br = [din(n_, [L, 1, 1] if SMALLW else [L, 1024, D]) for n_ in ("w_branch_ret", "w_branch_conv", "w_branch_sb")]
    w_out = din("w_out", [L, 1, 1] if SMALLW else [L, D, D])
    w_pg = din("w_ple_gate", [L, 1, 1] if SMALLW else [L, D, D])
    w_pu = din("w_ple_up", [L, 256, D])
    gam_in = din("gam", [128, L * 4 * 16])
    retgn_in = din("ret_gn", [L, 1024])
    sbq_in = din("sb_q_norm", [L, 128])
    sbk_in = din("sb_k_norm", [L, 128])
    sbb_in = din("sb_bias", [L, H])
    convw_in = din("convw", [128, L * 8 * 3])
    flag_in = din("flag", [128, 1])
    c_idf = din("c_idf", [128, 128])
    c_idb = din("c_idb", [128, 128], BF16)
    c_ones = din("c_ones", [128, 128], BF16)
    c_m01 = din("c_m01", [128, 128])
    c_msb = din("c_msb", [128, 4 * 512])
    c_nu = din("c_nu", [128, 128], BF16)
    c_nl = din("c_nl", [128, 128], BF16)
    c_usf = din("c_usf", [128, 128])
    c_cos = din("c_cos", [128, 9 * 64])
    c_sin = din("c_sin", [128, 9 * 64])
    c_dec = din("c_dec", [128, 4 * 9 * 8])
    c_oh = din("c_oh", [128, 64])

    y_out = dout("y", [NT, D])
    nk_out = dout("nk", [L, NT, 1024])
    nv_out = dout("nv", [L, NT, 1024])
    rs_out = dout("rs", [L, H, DH, DH])
    rss_out = dout("rss", [L, NS, H, DH, DH])
    cs_out = dout("cs", [L, 2, 1024])
    css_out = dout("css", [L, NS * 2, 1024])

    qt_scr = dscr("qt_scr", [L, H * 128, NT], BF16)
    rqt = dscr("rqt", [L, 9, 128, 1024], BF16)
    rkt = dscr("rkt", [L, 9, 128, 1024], BF16)
    rkd = dscr("rkd", [L, 9, 128, 1024], BF16)
    rktot = dscr("rktot", [L, 9, 128, 1024], BF16)
    rvs = dscr("rvs", [L, 9, 128, 1024], BF16)
    mg_scr = dscr("mg_scr", [L, 16, 128, NT], BF16)
    qs_scr = dscr("qs_scr", [L, NS, 128], F32)


    def sb(name, shape, dt):
        return es.enter_context(nc.sbuf_tensor(name, list(shape), dt))

    K = Ctx(nc, es)
    es.enter_context(nc.allow_non_contiguous_dma(reason="small strided layouts"))
    K.add_engine("pe", nc.tensor)
    K.add_engine("act", nc.scalar)
    K.add_engine("dve", nc.vector)
    K.add_engine("pool", nc.gpsimd)
    K.add_engine("sp", nc.sync)
    K.add_dma_sems(24)
    ccsem = es.enter_context(nc.semaphore("ccsem"))
    cccount = [0]
    ccevs = []

    def barrier():
        evs = [(e["sem"], e["count"]) for e in K.engs.values()] + [(s[0], s[1]) for s in K.dsems]
        evs.extend(ccevs)
        for e in K.engs.values():
            K._wait(e, evs)

    X = sb("X", [128, 16 * NT], F32)
    XN = sb("XN", [128, 16 * NT], BF16)
    BIG = sb("BIG", [128, 11 * NT], BF16)
    WS = [sb("WS%d" % i, [128, 4096], BF16) for i in range(3)]
    WSD = [Dep("ws%d" % i) for i in range(3)]
    T = [sb("T%d" % i, [128, NT], F32) for i in range(4)]
    TD = [Dep("t%d" % i) for i in range(4)]
    SQ = [sb("SQ%d" % i, [128, 512], BF16) for i in range(2)]
    SQB = [sb("SQB%d" % i, [128, 512], BF16) for i in range(4)]
    SQBD = [Dep("sqb%d" % i) for i in range(4)]
    SQD = [Dep(), Dep()]
    PS = [es.enter_context(nc.psum_tensor("PS%d" % i, [128, 512], F32)) for i in range(8)]
    PD = [Dep("ps%d" % i, excl=True) for i in range(8)]
    IDF = sb("IDF", [128, 128], F32)
    IDB = sb("IDB", [128, 128], BF16)
    ONES = sb("ONES", [128, 128], BF16)
    M01 = sb("M01", [128, 128], F32)
    MSB = sb("MSB", [128, 2048], F32)
    NU = sb("NU", [128, 128], BF16)
    NL = sb("NL", [128, 128], BF16)
    USF = sb("USF", [128, 128], F32)
    COS = sb("COS", [128, 9 * 64], F32)
    SIN = sb("SIN", [128, 9 * 64], F32)
    DEC = sb("DEC", [128, 4 * 72], F32)
    OH = sb("OH", [128, 64], F32)
    GAM = sb("GAM", [128, L * 64], F32)
    CONVW = sb("CONVW", [128, L * 24], F32)
    FLAG = sb("FLAG", [128, 1], F32)
    RETGN = sb("RETGN", [128, 1024], F32)
    SBQ = sb("SBQ", [128, 128], F32)
    SBK = sb("SBK", [128, 128], F32)
    SBB = sb("SBB", [128, H], F32)
    SBBP = sb("SBBP", [128, H], F32)
    SM = sb("SM", [128, 64], F32)
    SMD = Dep("sm")
    CD = Dep("consts")

    Xv = X[:].rearrange("p (c t) -> p c t", c=16)
    XNv = XN[:].rearrange("p (c t) -> p c t", c=16)
    XD = [Dep("x%d" % i) for i in range(16)]
    XND = Dep("xn")
    BIGD = Dep("big")

    def mm(out, lhsT, rhs, start, stop, reads, writes, inc=None):
        return K.op("pe", lambda e: e.matmul(out, lhsT=lhsT, rhs=rhs, start=start, stop=stop), reads, writes,
                    inc=True if inc is None else inc)

    def tr(out, in_, ident, reads, writes):
        return K.op("pe", lambda e: e.transpose(out, in_, ident), reads, writes)

    def act(out, in_, func, reads, writes, scale=None, bias=None, accum=None):
        kw = {}
        if scale is not None:
            kw["scale"] = scale
        if bias is not None:
            kw["bias"] = bias
        if accum is not None:
            kw["accum_out"] = accum
        return K.op("act", lambda e: e.activation(out=out, in_=in_, func=func, **kw), reads, writes)

    def tt(out, in0, in1, op, reads, writes, en="dve"):
        return K.op(en, lambda e: e.tensor_tensor(out=out, in0=in0, in1=in1, op=op), reads, writes)

    def stt(out, in0, scalar, in1, op0, op1, reads, writes):
        return K.op("dve", lambda e: e.scalar_tensor_tensor(out=out, in0=in0, scalar=scalar, in1=in1, op0=op0, op1=op1),
                    reads, writes)

    def ts(out, in0, s1, op0, reads, writes, s2=None, op1=None, en="dve"):
        if op1 is None:
            return K.op(en, lambda e: e.tensor_scalar(out=out, in0=in0, scalar1=s1, scalar2=None, op0=op0), reads, writes)
        return K.op(en, lambda e: e.tensor_scalar(out=out, in0=in0, scalar1=s1, scalar2=s2, op0=op0, op1=op1), reads, writes)

    def cp(out, in_, reads, writes, en="dve"):
        if en == "act":
            return act(out, in_, AF.Copy, reads, writes)
        return K.op(en, lambda e: e.tensor_copy(out, in_), reads, writes)

    slab_i = [0]

    def load_slab(wap, r0, nk, c0, ncols, off=0, more=False, same=False):
        if not (more or same):
            slab_i[0] = (slab_i[0] + 1) % 3
        i = slab_i[0]
        v = WS[i][:, off:off + nk * ncols].rearrange("p (k n) -> p k n", k=nk)
        src = wap[r0:r0 + nk * 128, c0:c0 + ncols].rearrange("(k p) n -> p k n", p=128)
        K.dma("pool", v, src, writes=[WSD[i]], more=more)
        return v, WSD[i]

    for (t_, a_) in [(IDF, c_idf), (IDB, c_idb), (ONES, c_ones), (M01, c_m01), (MSB, c_msb), (NU, c_nu), (NL, c_nl),
                     (USF, c_usf), (COS, c_cos), (SIN, c_sin), (DEC, c_dec), (OH, c_oh), (GAM, gam_in),
                     (CONVW, convw_in), (FLAG, flag_in)]:
        K.dma("sp", t_[:], a_[:, :], writes=[CD], more=True)
    COSv = COS[:].rearrange("p (c e) -> p c e", c=9)
    SINv = SIN[:].rearrange("p (c e) -> p c e", c=9)
    DECv = DEC[:].rearrange("p (k c h) -> p k c h", k=4, c=9)
    DINV, DDEC, DTOT, DQ = 0, 1, 2, 3

    BIGF = BIG[:].bitcast(F32)
    for ti, (t0, r) in enumerate(TCS):
        stg = BIGF[:, (ti % 2) * 2048:(ti % 2) * 2048 + 2048]
        sd = SQD[ti % 2]
        K.dma("sp", stg[:r, :], x_in[t0:t0 + r, :], writes=[sd])
        for g in range(4):
            pb = (ti * 4 + g) % 4
            for j in range(4):
                fc = g * 4 + j
                tr(PS[pb][:, j * 128:j * 128 + r], stg[:r, fc * 128:(fc + 1) * 128], IDF[:r, :r], [sd, CD], [PD[pb]])
            cp(Xv[:, g * 4:g * 4 + 4, t0:t0 + r], PS[pb][:, :].rearrange("p (j t) -> p j t", j=4)[:, :, :r],
               [PD[pb]], [XD[g * 4 + j] for j in range(4)], en=("act" if g % 2 else "dve"))

    def rmsnorm(gidx):
        for (c0, n) in CTS:
            for fc in range(16):
                act(SQ[fc % 2][:, :n], Xv[:, fc, c0:c0 + n], AF.Square, [XD[fc]], [SQD[fc % 2]])
                mm(PS[7][:, :n], ONES[:], SQ[fc % 2][:, :n], fc == 0, fc == 15, [SQD[fc % 2], CD], [PD[7]])
            act(T[0][:, :n], PS[7][:, :n], AF.Ln, [PD[7]], [TD[0]], scale=1.0 / D, bias=EPS)
            act(T[1][:, :n], T[0][:, :n], AF.Exp, [TD[0]], [TD[1]], scale=-0.5)
            for fc in range(16):
                stt(XNv[:, fc, c0:c0 + n], Xv[:, fc, c0:c0 + n], GAM[:, gidx * 16 + fc:gidx * 16 + fc + 1], T[1][:, :n],
                    ALU.mult, ALU.mult, [XD[fc], TD[1], CD], [XND])

    HBv = BIG[:].rearrange("p (i t) -> p i t", i=11)

    def ffn(l, which):
        if DEBUG_SMALL:
            return
        wgu = w_gu[which][l]
        wdn = w_dn[which][l]
        it = 0
        for q4 in range(4):
            for i in range(11):
                hc = q4 * 11 + i
                sv, sdp = load_slab(wgu, 0, 16, hc * 128, 128)
                load_slab(wgu, 0, 16, DFF + hc * 128, 128, off=2048, more=True)
                uv = WS[slab_i[0]][:, 2048:4096].rearrange("p (k n) -> p k n", k=16)
                for (c0, n) in CTS:
                    pg, pu = (it % 2) * 2, (it % 2) * 2 + 1
                    it += 1
                    for kc in range(16):
                        mm(PS[pg][:, :n], sv[:, kc, :], XNv[:, kc, c0:c0 + n], kc == 0, kc == 15, [sdp, XND], [PD[pg]], inc=(kc == 15))
                    for kc in range(16):
                        mm(PS[pu][:, :n], uv[:, kc, :], XNv[:, kc, c0:c0 + n], kc == 0, kc == 15, [sdp, XND], [PD[pu]], inc=(kc == 15))
                    tq = it % 2
                    act(T[tq][:, :n], PS[pg][:, :n], AF.Silu, [PD[pg]], [TD[tq]])
                    tt(HBv[:, i, c0:c0 + n], T[tq][:, :n], PS[pu][:, :n], ALU.mult, [TD[tq], PD[pu]], [BIGD])
            for og in range(8):
                dv, ddp = load_slab(wdn, q4 * 1408, 11, og * 256, 256)
                for o2 in range(2):
                    oc = og * 2 + o2
                    for (c0, n) in CTS:
                        pdn = 4 + (it % 2)
                        it += 1
                        for i in range(11):
                            mm(PS[pdn][:, :n], dv[:, i, o2 * 128:(o2 + 1) * 128], HBv[:, i, c0:c0 + n], i == 0, i == 10,
                               [ddp, BIGD], [PD[pdn]], inc=(i == 10))
                        stt(Xv[:, oc, c0:c0 + n], PS[pdn][:, :n], 0.5, Xv[:, oc, c0:c0 + n], ALU.mult, ALU.add,
                            [PD[pdn]], [XD[oc]])

    XB = X[:].bitcast(BF16)
    YAv = XB[:, 0:8 * NT].rearrange("p (c t) -> p c t", c=8)
    YBv = XB[:, 8 * NT:16 * NT].rearrange("p (c t) -> p c t", c=8)
    YCv = XB[:, 16 * NT:24 * NT].rearrange("p (c t) -> p c t", c=8)
    ATT = XB[:, 24 * NT:32 * NT]
    ATTF = X[:, 12 * NT:16 * NT]
    YAD, YBD, YCD = Dep("ya"), Dep("yb"), Dep("yc")
    KT2 = BIG[:, 0:2 * NT].rearrange("p (h t) -> p h t", h=2)
    KT2D = Dep("kt2")
    SK = BIGF[:, 1032:2056]
    SV = BIGF[:, 2056:3080]
    SQT = BIGF[:, 3080:3144].rearrange("p (h s) -> p h s", h=8)
    SKD, SVD, SQTD = Dep("sk"), Dep("sv"), Dep("sqt")
    RB = BIG[:, 6288:6288 + 4096]
    RBD = [Dep("rb%d" % i) for i in range(4)]
    UBf = BIGF[:, 5192:5192 + 484]
    UB = sb("UB", [128, NT + 2], F32)
    UBD = Dep("ub")
    IDX = sb("IDX", [128, 8], I32)
    IDXALL = sb("IDXALL", [128, 64], I32)
    OHC = sb("OHC", [128, 8], F32)
    BOWN = sb("BOWN", [128, 1], F32)
    QSEL = sb("QSEL", [128, 128], F32)
    QSELD = Dep("qsel")
    SM2 = sb("SM2", [128, 64], F32)
    SM3 = sb("SM3", [128, 64], F32)
    SM2D, SM3D = Dep("sm2"), Dep("sm3")
    x_scr = dscr("x_scr", [128, 16 * NT], F32)
    XSD = Dep("xscr")
    CC1D, CC2D, CC3D, QTD, RQD, MGD, QSD = Dep(), Dep(), Dep(), Dep(), Dep(), Dep(), Dep()
    CC1O, CC2O, CC3O, CC1VO = Dep(), Dep(), Dep(), Dep()
    ohc_in = din("ohc", [128, 8])
    K.dma("sp", OHC[:], ohc_in[:, :], writes=[CD], more=True)
    K.dma("sp", IDX[:], ptab.rearrange("s j -> j s"), writes=[CD], more=True)
    for sg in range(8):
        ts(IDXALL[:, sg * 8:(sg + 1) * 8], IDX[:], 8.0, ALU.mult, [CD], [SMD], s2=float(sg), op1=ALU.add)
    ck2 = ck.rearrange("l n (g s) d -> (l n g) (s d)", g=8)
    cv2 = cv.rearrange("l n (g s) d -> (l n g) (s d)", g=8)
    IDXL1 = sb("IDXL1", [128, 64], I32)
    ts(IDXL1[:], IDXALL[:], float((8 if DEBUG_SMALL else NPOOL) * 8), ALU.add, [SMD], [SMD])
    IDXL = [IDXALL, IDXL1]
    SC = float(DH) ** -0.5
    PSB = [PS[i][:].bitcast(BF16) for i in range(8)]

    def cc(kind_groups, ins, outs, rd, wr):
        sem = es.enter_context(nc.semaphore("ccs%d" % cccount[0]))
        cccount[0] += 1
        e = K.engs["pool"]
        K._wait(e, K._events([rd], [wr]))
        nc.gpsimd.collective_compute("AllGather", ALU.bypass, replica_groups=kind_groups,
                                     ins=[ins.ap().opt()], outs=[outs.ap().opt()]).then_inc(sem)
        ev = (sem, 1)
        K.trace.setdefault("pool", []).append(("inc", id(sem), 1, "cc"))
        ccevs.append(ev)
        rd.r.append(ev)
        wr.w = [ev]
        wr.r = []

    PAIRS = [[0, 1], [2, 3], [4, 5], [6, 7]]

    def proj_tm(l, col0, hp, tc_fn):
        sv_, sdp = load_slab(w_in[l], 0, 16, col0 + hp * 256, 256)
        for tci, (t0, r) in enumerate(TCS):
            pi = tci % 2
            for kc in range(16):
                mm(PS[pi][:r, 0:256], XNv[:, kc, t0:t0 + r], sv_[:, kc, :], kc == 0, kc == 15, [sdp, XND], [PD[pi]])
            tc_fn(tci, t0, r, PS[pi], PD[pi])

    def rstd_small(r, n):
        act(SM2[:r, 0:n], SM[:r, 0:n], AF.Ln, [SMD], [SM2D], scale=1.0 / DH, bias=EPS)
        act(SM3[:r, 0:n], SM2[:r, 0:n], AF.Exp, [SM2D], [SM3D], scale=-0.5)

    def mixer_A(l):
        K.dma("sp", RETGN[:], retgn_in[l].partition_broadcast(128), writes=[CD])
        K.dma("sp", SBQ[:], sbq_in[l].partition_broadcast(128), writes=[CD], more=True)
        K.dma("sp", SBK[:], sbk_in[l].partition_broadcast(128), writes=[CD], more=True)
        K.dma("sp", SBB[:], sbb_in[l].partition_broadcast(128), writes=[CD], more=True)
        ts(SM[:, 0:1], FLAG[:, 0:1], -1.0, ALU.add, [CD], [SMD], s2=1.0e4, op1=ALU.mult)
        ts(SBBP[:], SBB[:], SM[:, 0:1], ALU.add, [CD, SMD], [SM2D])
        tt(SM2[:, 0:8], SBB[:], OHC[:], ALU.mult, [CD], [SM2D])
        K.op("dve", lambda e: e.tensor_reduce(out=BOWN[:], in_=SM2[:, 0:8], axis=AX.X, op=ALU.add), [SM2D], [SM3D])

        if stop_after == "mixA0":
            return
        for kind, col0 in (("k", 7168), ("v", 8192), ("q", 6144)):
            if kind not in os.environ.get("A1KINDS", "kvq"):
                continue
            for hp in range(4):
                def fn(tci, t0, r, P, Pd, kind=kind, hp=hp):
                    tb = 2 + (tci % 2)
                    A1M = int(os.environ.get("A1M", "9"))
                    if A1M == 0:
                        cp(T[tb][:r, 0:256], P[:r, 0:256], [Pd], [TD[tb]], en="act")
                        return
                    if kind == "v":
                        cp(T[tb][:r, 0:256], P[:r, 0:256], [Pd], [TD[tb]], en="act")
                        if "n" not in os.environ.get("VSKIP", ""):
                            K.dma("sp", nv_out[l, t0:t0 + r, hp * 256:(hp + 1) * 256], T[tb][:r, 0:256], reads=[TD[tb]])
                        if tci < 8 and "c" not in os.environ.get("VSKIP", ""):
                            cp(SQ[tci % 2][:r, 0:256], P[:r, 0:256], [Pd], [SQD[tci % 2]])
                            K.dma("sp", cc1v_in[l][hp * 256:(hp + 1) * 256, :].rearrange("(t a) b -> t (a b)", a=2)[t0:t0 + r, :].rearrange("t (a b) -> t a b", a=1)[:, 0, :] if False else cc1v_in[l].rearrange("(g t) c -> g (t c)", g=4)[hp, t0 * 256:(t0 + r) * 256].rearrange("(t c) -> t c", c=256), SQ[tci % 2][:r, 0:256],
                                  reads=[SQD[tci % 2]], writes=[CC1D], more=True)
                        return
                    G = SBK if kind == "k" else SBQ
                    act(T[0][:r, 0:256], P[:r, 0:256], AF.Square, [Pd], [TD[0]])
                    K.op("dve", lambda e: e.tensor_reduce(out=SM[:r, 0:2], in_=T[0][:r, 0:256].rearrange("p (h e) -> p h e", h=2),
                                                          axis=AX.X, op=ALU.add), [TD[0]], [SMD])
                    if A1M == 1:
                        return
                    rstd_small(r, 2)
                    for hh in range(2):
                        stt(T[tb][:r, hh * 128:(hh + 1) * 128], P[:r, hh * 128:(hh + 1) * 128], SM3[:r, hh:hh + 1], G[:r, :],
                            ALU.mult, ALU.mult, [Pd, SM3D, CD], [TD[tb]])
                    if A1M == 2:
                        return
                    if kind == "k":
                        K.dma("sp", nk_out[l, t0:t0 + r, hp * 256:(hp + 1) * 256], T[tb][:r, 0:256], reads=[TD[tb]])
                    elif tci == 8:
                        for hh in range(2):
                            h = hp * 2 + hh
                            if h == 0:
                                ts(QSEL[:r, :], T[tb][:r, 0:128], OHC[:r, 0:1], ALU.mult, [TD[tb], CD], [QSELD])
                            else:
                                stt(QSEL[:r, :], T[tb][:r, hh * 128:(hh + 1) * 128], OHC[:r, h:h + 1], QSEL[:r, :], ALU.mult, ALU.add,
                                    [TD[tb], CD], [QSELD])
                    if A1M == 3:
                        return
                    sq_ = SQ[tci % 2]
                    cp(sq_[:r, 0:256], T[tb][:r, 0:256], [TD[tb]], [SQD[tci % 2]], en="act")
                    pt = 4 + (tci % 2)
                    for hh in range(2):
                        tr(PSB[pt][:, hh * 128:hh * 128 + r], sq_[:r, hh * 128:(hh + 1) * 128], IDB[:r, :r], [SQD[tci % 2], CD], [PD[pt]])
                    cp(KT2[:, :, t0:t0 + r], PSB[pt][:, 0:256].rearrange("p (h t) -> p h t", h=2)[:, :, :r], [PD[pt]], [KT2D])
                proj_tm(l, col0, hp, fn)
                if kind == "k":
                    for hh in range(2):
                        h = hp * 2 + hh
                        K.dma("sp", cc1k_in[l][h * 128:(h + 1) * 128, :], KT2[:, hh, 0:1024], reads=[KT2D], writes=[CC1D], more=True)
                elif kind == "q":
                    for hh in range(2):
                        h = hp * 2 + hh
                        K.dma("sp", qt_scr[l, h * 128:(h + 1) * 128, :], KT2[:, hh, :], reads=[KT2D], writes=[QTD], more=True)
        K.dma("sp", qs_scr[l], QSEL[:8, :], reads=[QSELD], writes=[QSD])
        if stop_after == "mixA1":
            return

        for kind, col0 in (("k", 1024), ("v", 2048), ("q", 0)):
            for hp in range(4):
                def fn(tci, t0, r, P, Pd, kind=kind, hp=hp):
                    hs = slice(hp * 256, (hp + 1) * 256)
                    if kind == "v":
                        if tci == 8:
                            cp(SV[:r, hs], P[:r, 0:256], [Pd], [SVD], en="act")
                        else:
                            cp(SQ[tci % 2][:r, 0:256], P[:r, 0:256], [Pd], [SQD[tci % 2]], en="act")
                            K.dma("sp", rvs[l, tci, :r, hs], SQ[tci % 2][:r, 0:256], reads=[SQD[tci % 2]], writes=[RQD], more=True)
                        return
                    P4 = P[:r, 0:256].rearrange("p (a e) -> p a e", a=4)
                    cosb = COSv[:r, tci, :].unsqueeze(1).to_broadcast([r, 4, 64])
                    sinb = SINv[:r, tci, :].unsqueeze(1).to_broadcast([r, 4, 64])
                    tt(T[0][:r, 0:256].rearrange("p (a e) -> p a e", a=4), P4, cosb, ALU.mult, [Pd, CD], [TD[0]])
                    tt(T[1][:r, 0:256].rearrange("p (a e) -> p a e", a=4), P4, sinb, ALU.mult, [Pd, CD], [TD[1]])
                    A5 = T[0][:r, 0:256].rearrange("p (h f e) -> p h f e", h=2, f=2)
                    B5 = T[1][:r, 0:256].rearrange("p (h f e) -> p h f e", h=2, f=2)
                    R5 = T[2][:r, 0:256].rearrange("p (h f e) -> p h f e", h=2, f=2)
                    tt(R5[:, :, 0, :], A5[:, :, 0, :], B5[:, :, 1, :], ALU.subtract, [TD[0], TD[1]], [TD[2]])
                    tt(R5[:, :, 1, :], A5[:, :, 1, :], B5[:, :, 0, :], ALU.add, [TD[0], TD[1]], [TD[2]])
                    R3 = T[2][:r, 0:256].rearrange("p (h e) -> p h e", h=2)

                    def dec(k_):
                        return DECv[:r, k_, tci, hp * 2:hp * 2 + 2].unsqueeze(2).to_broadcast([r, 2, 128])
                    sq_ = SQ[tci % 2]
                    if kind == "q":
                        if tci == 8:
                            for hh in range(2):
                                tr(PS[6][:, hh * 8:hh * 8 + r], T[2][:r, hh * 128:(hh + 1) * 128], IDF[:r, :r], [TD[2], CD], [PD[6]])
                            cp(SQT[:, hp * 2:hp * 2 + 2, :], PS[6][:, 0:16].rearrange("p (h s) -> p h s", h=2), [PD[6]], [SQTD])
                        tt(sq_[:r, 0:256].rearrange("p (h e) -> p h e", h=2), R3, dec(DQ), ALU.mult, [TD[2], CD], [SQD[tci % 2]])
                    else:
                        if tci == 8:
                            tt(SK[:r, hs].rearrange("p (h e) -> p h e", h=2), R3, dec(DDEC), ALU.mult, [TD[2], CD], [SKD])
                            return
                        tt(T[3][:r, 0:256].bitcast(BF16)[:, 0:256].rearrange("p (h e) -> p h e", h=2), R3, dec(DDEC), ALU.mult,
                           [TD[2], CD], [TD[3]])
                        K.dma("sp", rkd[l, tci, :r, hs], T[3][:r, 0:256].bitcast(BF16)[:, 0:256], reads=[TD[3]], writes=[RQD], more=True)
                        tt(T[3][:r, 256:512].bitcast(BF16)[:, 0:256].rearrange("p (h e) -> p h e", h=2), R3, dec(DTOT), ALU.mult,
                           [TD[2], CD], [TD[3]])
                        K.dma("sp", rktot[l, tci, :r, hs], T[3][:r, 256:512].bitcast(BF16)[:, 0:256], reads=[TD[3]], writes=[RQD], more=True)
                        tt(sq_[:r, 0:256].rearrange("p (h e) -> p h e", h=2), R3, dec(DINV), ALU.mult, [TD[2], CD], [SQD[tci % 2]])
                    pt = 4 + (tci % 2)
                    for hh in range(2):
                        tr(PSB[pt][:, hh * 128:hh * 128 + r], sq_[:r, hh * 128:(hh + 1) * 128], IDB[:r, :r], [SQD[tci % 2], CD], [PD[pt]])
                    kb = (tci % 2)
                    cp(KT2[:, kb, 0:256].rearrange("p (h t) -> p h t", h=2)[:, :, :r],
                       PSB[pt][:, 0:256].rearrange("p (h t) -> p h t", h=2)[:, :, :r], [PD[pt]], [KT2D], en="act")
                    dst = rqt if kind == "q" else rkt
                    K.dma("sp", dst[l, tci, :, hs].rearrange("p (h t) -> p h t", h=2)[:, :, :r],
                          KT2[:, kb, 0:256].rearrange("p (h t) -> p h t", h=2)[:, :, :r], reads=[KT2D], writes=[RQD], more=True)
                proj_tm(l, col0, hp, fn)
        for tci in range(8):
            b0, b1 = RB[:, 0:1024], RB[:, 1024:2048]
            K.dma("sp", b0, rktot[l, tci], reads=[RQD], writes=[RBD[0]])
            K.dma("sp", b1, rvs[l, tci], reads=[RQD], writes=[RBD[1]])
            for h in range(8):
                mm(PS[h // 4][:, (h % 4) * 128:(h % 4 + 1) * 128], b0[:, h * 128:(h + 1) * 128], b1[:, h * 128:(h + 1) * 128],
                   True, True, [RBD[0], RBD[1]], [PD[h // 4]])
            for hf_ in range(2):
                if tci == 0:
                    cp(T[0][:, hf_ * 512:(hf_ + 1) * 512], PS[hf_][:, :], [PD[hf_]], [TD[0]])
                else:
                    tt(T[0][:, hf_ * 512:(hf_ + 1) * 512], T[0][:, hf_ * 512:(hf_ + 1) * 512], PS[hf_][:, :], ALU.add, [PD[hf_]], [TD[0]])
        K.dma("sp", cc2_in[l][0:1024, :].rearrange("(h d) e -> d h e", h=8), T[0][:, 0:1024].rearrange("p (h e) -> p h e", h=8),
              reads=[TD[0]], writes=[CC2D])
        if stop_after == "mixA2":
            return
        CHt = UBf[:, 0:16].rearrange("p (j c) -> p j c", j=2)
        CCt = UBf[:, 16:32].rearrange("p (j c) -> p j c", j=2)
        for which, col0, dstt in ((0, 3072, CHt), (1, 5120, CCt)):
            for sl in range(4):
                sv_, sdp = load_slab(w_in[l], 0, 16, col0 + sl * 256, 256)
                for o2 in range(2):
                    c = sl * 2 + o2
                    for kc in range(16):
                        mm(PS[6][:, 0:2], sv_[:, kc, o2 * 128:(o2 + 1) * 128], XNv[:, kc, 1022:1024], kc == 0, kc == 15, [sdp, XND], [PD[6]])
                    cp(dstt[:, :, c], PS[6][:, 0:2], [PD[6]], [UBD])
        UT = UBf[:, 32:48].rearrange("p (j c) -> p j c", j=2)
        tt(UT, CHt, CCt, ALU.mult, [UBD], [UBD])
        for j in range(2):
            K.dma("sp", cc2_in[l][1024 + j * 8:1032 + j * 8, :].rearrange("c p -> p c"), UT[:, j, :], reads=[UBD], writes=[CC2D], more=True)
            K.dma("sp", cs_out[l, j].rearrange("(c p) -> p c", c=8), UT[:, j, :], reads=[UBD])
        if stop_after == "mixA3":
            return
        cc(PAIRS, cc1k_in_t[l], cc1k_out_t[l], CC1D, CC1O)
        cc(PAIRS, cc1v_in_t[l], cc1v_out_t[l], CC1D, CC1VO)
        cc(PAIRS, cc2_in_t[l], cc2_out_t[l], CC2D, CC2O)

    def vview(ap2d, h):
        return ap2d.rearrange("(g t) c -> g (t c)", g=4)[h // 2].rearrange("(t c) -> t c", c=256)[
            :, (h % 2) * 128:(h % 2) * 128 + 128].rearrange("(c p) e -> p c e", p=128)

    def groupnorm_chunk(tci, t0, r, srcs=None, sdeps=None):
        if srcs is None:
            srcs = [PS[2], PS[3]]
            sdeps = [PD[2], PD[3]]
        for hf_ in range(2):
            act(T[0][:r, hf_ * 512:(hf_ + 1) * 512], srcs[hf_][:r, 0:512], AF.Square, [sdeps[hf_]], [TD[0]])
        K.op("dve", lambda e: e.tensor_reduce(out=SM[:r, 0:8], in_=T[0][:r, 0:1024].rearrange("p (h e) -> p h e", h=8),
                                              axis=AX.X, op=ALU.add), [TD[0]], [SMD])
        rstd_small(r, 8)
        yat = RB[:, 2048:3072]
        for h in range(8):
            stt(yat[:r, h * 128:(h + 1) * 128], srcs[h // 4][:r, (h % 4) * 128:(h % 4 + 1) * 128], SM3[:r, h:h + 1],
                RETGN[:r, h * 128:(h + 1) * 128], ALU.mult, ALU.mult, [sdeps[h // 4], SM3D, CD], [RBD[2]])
        for h in range(8):
            tr(PSB[6][:, h * 128:h * 128 + r], yat[:r, h * 128:(h + 1) * 128], IDB[:r, :r], [RBD[2], CD], [PD[6]])
        cp(YAv[:, :, t0:t0 + r], PSB[6][:, 0:1024].rearrange("p (h t) -> p h t", h=8)[:, :, :r], [PD[6]], [YAD])

    def mixer_B(l):
        AK = ATT[:, 0:2048]
        AVv = ATT[:, 2048:4096].rearrange("p (c e) -> p c e", c=16)
        AQ = ATT[:, 4096:5120]
        AKD, AVD, AQD = Dep("ak"), Dep("av"), Dep("aq")
        THD = [[Dep("th%d_%d" % (k_, p_)) for p_ in range(2)] for k_ in range(4)]
        for h in range(8):
            K.dma("sp", AK[:, 0:1024], cc1k_out[l][h * 128:(h + 1) * 128, :], reads=[CC1O], writes=[AKD])
            K.dma("sp", AK[:, 1024:2048], cc1k_in[l][h * 128:(h + 1) * 128, :], reads=[CC1D], writes=[AKD], more=True)
            K.dma("sp", AVv[:, 0:8, :], vview(cc1v_out[l][0:1024, :], h), reads=[CC1VO], writes=[AVD])
            K.dma("sp", AVv[:, 8:16, :], vview(cc1v_in[l], h), reads=[CC1D], writes=[AVD], more=True)
            K.dma("sp", AQ, qt_scr[l, h * 128:(h + 1) * 128, 0:1024], reads=[QTD], writes=[AQD])
            for Q in range(2):
                blocks = [(True, kb) for kb in range(4 * Q + 3, -1, -1)] + [(False, kb) for kb in range(7, -1, -1)]
                for bi, (own, kb) in enumerate(blocks):
                    ci = 8 + kb if own else kb
                    first, last = bi == 0, bi == len(blocks) - 1
                    par = bi % 2
                    hs_ = slice(par * 512, par * 512 + 512)
                    E_, SPf, EA_, R_ = T[0][:, hs_], T[1][:, hs_], T[2][:, hs_], T[3][:, hs_]
                    Ed, SPd, EAd, Rd = THD[0][par], THD[1][par], THD[2][par], THD[3][par]
                    spb, spbd = SQB[par], SQBD[par]
                    ab, abd = SQB[2 + par], SQBD[2 + par]
                    pz, pa, pb = par, 2 + 3 * par, 4 + 2 * par
                    mm(PS[pz][:, :512], AK[:, ci * 128:(ci + 1) * 128], AQ[:, Q * 512:(Q + 1) * 512], True, True, [AKD, AQD], [PD[pz]])
                    bias = (SBB if own else SBBP)[:, h:h + 1]
                    act(E_, PS[pz][:, :512], AF.Exp, [PD[pz], CD, SM2D], [Ed], scale=SC, bias=bias)
                    act(SPf, E_, AF.Ln, [Ed], [SPd], bias=1.0)
                    r_ = kb - 4 * Q
                    diag = own and r_ >= 0
                    if diag:
                        tt(spb[:, :512], SPf, MSB[:, r_ * 512:(r_ + 1) * 512], ALU.mult, [SPd, CD], [spbd])
                    else:
                        cp(spb[:, :512], SPf, [SPd], [spbd])
                    mm(PS[pa][:, :512], NU[:], spb[:, :512], True, True, [spbd, CD], [PD[pa]])
                    mm(PS[pb][:, :512], ONES[:], spb[:, :512], True, True, [spbd, CD], [PD[pb]])
                    act(R_, SPf, AF.Exp, [SPd], [Rd], scale=-1.0)
                    act(EA_, PS[pa][:, :512], AF.Exp, [PD[pa]], [EAd])
                    tt(R_, R_, E_, ALU.mult, [Ed], [Rd])
                    if diag:
                        tt(R_, R_, MSB[:, r_ * 512:(r_ + 1) * 512], ALU.mult, [CD], [Rd])
                    if not first:
                        tt(EA_, EA_, UB[:, 0:512], ALU.mult, [UBD], [EAd])
                    if not last:
                        if first:
                            act(UB[:, 0:512], PS[pb][:, :512], AF.Exp, [PD[pb]], [UBD], scale=-1.0)
                        else:
                            act(SPf, PS[pb][:, :512], AF.Exp, [PD[pb]], [SPd], scale=-1.0)
                            tt(UB[:, 0:512], UB[:, 0:512], SPf, ALU.mult, [SPd], [UBD], en="pool")
                    tt(ab[:, :512], R_, EA_, ALU.mult, [Rd, EAd], [abd])
                    mm(PS[3][:, :512], AVv[:, ci, :], ab[:, :512], first, last, [AVD, abd], [PD[3]])
                cp(YCv[:, h, Q * 512:(Q + 1) * 512], PS[3][:, :512], [PD[3]], [YCD], en="act")
        if stop_after == "mixB2":
            return
        barrier()
        KS = [ATTF[:, 0:2048], ATTF[:, 2048:4096]]
        KSD = [Dep("ks0"), Dep("ks1")]
        QBC = RB[:, 0:2048].bitcast(F32)
        K.dma("sp", QBC, qs_scr[l].rearrange("s d -> (s d)").partition_broadcast(128), reads=[QSD], writes=[RBD[0]])
        ZS = T[0][:, 0:1024].rearrange("p (s j) -> p s j", s=8)
        gi = 0
        for s in range(8):
            for sg in range(8):
                ks, ksd = KS[gi % 2], KSD[gi % 2]
                gi += 1
                K.dma("pool", None, None, reads=[CD, SMD], writes=[ksd],
                      fn=lambda e, ks=ks, sg=sg, s=s: e.indirect_dma_start(
                          out=ks, out_offset=None, in_=ck2,
                          in_offset=bass.IndirectOffsetOnAxis(ap=IDXL[l][:, sg * 8 + s:sg * 8 + s + 1], axis=0)))
                k3 = ks.rearrange("p (j d) -> p j d", j=16)
                tt(k3, k3, QBC[:, s * 128:(s + 1) * 128].unsqueeze(1).to_broadcast([128, 16, 128]), ALU.mult, [RBD[0]], [ksd])
                K.op("dve", lambda e, k3=k3, s=s, sg=sg: e.tensor_reduce(out=ZS[:, s, sg * 16:(sg + 1) * 16], in_=k3, axis=AX.X, op=ALU.add),
                     [ksd], [TD[0]])
        E_, SP_, A_, B_ = T[0][:, 0:1024], T[1][:, 0:1024], T[2][:, 0:1024], T[3][:, 0:1024]
        act(E_, E_, AF.Exp, [SM3D], [TD[0]], scale=SC, bias=BOWN[:, 0:1])
        act(SP_, E_, AF.Ln, [TD[0]], [TD[1]], bias=1.0)
        sp3 = SP_.rearrange("p (s j) -> p s j", s=8)
        K.op("dve", lambda e: e.tensor_reduce(out=SM[:, 0:8], in_=sp3, axis=AX.X, op=ALU.add), [TD[1]], [SMD])
        mm(PS[0][:, 0:8], USF[:], SM[:, 0:8], True, True, [SMD, CD], [PD[0]])
        cp(SM2[:, 0:8], PS[0][:, 0:8], [PD[0]], [SM2D])
        srcb, srcd = SP_, TD[1]
        for step, sh in enumerate((1, 2, 4, 8, 16, 32, 64)):
            dstb, dstd = (A_, TD[2]) if step % 2 == 0 else (B_, TD[3])
            s3 = srcb.rearrange("p (s j) -> p s j", s=8)
            d3 = dstb.rearrange("p (s j) -> p s j", s=8)
            tt(d3[:, :, 0:128 - sh], s3[:, :, 0:128 - sh], s3[:, :, sh:128], ALU.add, [srcd], [dstd])
            cp(d3[:, :, 128 - sh:128], s3[:, :, 128 - sh:128], [srcd], [dstd])
            srcb, srcd = dstb, dstd
        tt(B_, A_, SP_, ALU.subtract, [TD[2], TD[1]], [TD[3]])
        b3 = B_.rearrange("p (s j) -> p s j", s=8)
        tt(b3, b3, SM2[:, 0:8].unsqueeze(2).to_broadcast([128, 8, 128]), ALU.add, [SM2D], [TD[3]])
        act(A_, B_, AF.Exp, [TD[3]], [TD[2]], scale=-1.0)
        ts(B_, E_, 1.0, ALU.add, [TD[0]], [TD[3]])
        K.op("dve", lambda e: e.reciprocal(B_, B_), [], [TD[3]])
        tt(B_, B_, E_, ALU.mult, [TD[0]], [TD[3]])
        tt(B_, B_, A_, ALU.mult, [TD[2]], [TD[3]])
        AM = T[1][:, 0:1024].rearrange("p (j s) -> p j s", j=128)
        OHv = OH[:].rearrange("p (s t) -> p s t", s=8)
        for s in range(8):
            tt(AM, b3[:, s, :].unsqueeze(2).to_broadcast([128, 128, 8]), OHv[:, s, :].unsqueeze(1).to_broadcast([128, 128, 8]),
               ALU.mult, [TD[3], CD], [TD[1]])
            for sg in range(8):
                ks, ksd = KS[gi % 2], KSD[gi % 2]
                gi += 1
                K.dma("pool", None, None, reads=[CD, SMD], writes=[ksd],
                      fn=lambda e, ks=ks, sg=sg, s=s: e.indirect_dma_start(
                          out=ks, out_offset=None, in_=cv2,
                          in_offset=bass.IndirectOffsetOnAxis(ap=IDXL[l][:, sg * 8 + s:sg * 8 + s + 1], axis=0)))
                for j in range(16):
                    slot = sg * 16 + j
                    mm(PS[1][:8, 0:128], AM[:, slot, :], ks[:, j * 128:(j + 1) * 128], s == 0 and slot == 0, s == 7 and slot == 127,
                       [TD[1], ksd], [PD[1]])
        cp(SM2[:8, 0:128] if False else UBf[:8, 64:192], PS[1][:8, 0:128], [PD[1]], [UBD])
        K.dma("sp", cc3_in[l], UBf[:8, 64:192], reads=[UBD], writes=[CC3D])
        cc([list(range(8))], cc3_in_t[l], cc3_out_t[l], CC3D, CC3O)
        K.dma("sp", T[2][:64, 0:128], cc3_out[l], reads=[CC3O], writes=[TD[2]])
        tr(PS[0][:, 0:64], T[2][:64, 0:128], IDF[:64, :64], [TD[2], CD], [PD[0]])
        cp(YCv[:, :, 1024:1032], PS[0][:, 0:64].rearrange("p (h s) -> p h s", h=8), [PD[0]], [YCD])
        if stop_after == "mixB3":
            return
        barrier()
        SF = T[3][:, 0:1024].rearrange("p (h e) -> p h e", h=8)
        SBF = RB[:, 3072:4096].rearrange("p (h e) -> p h e", h=8)
        K.dma("sp", SF, cc2_out[l][0:1024, :].rearrange("(h d) e -> d h e", h=8), reads=[CC2O], writes=[TD[3]])
        ts(T[3][:, 0:1024], T[3][:, 0:1024], FLAG[:, 0:1], ALU.mult, [CD], [TD[3]])
        cp(RB[:, 3072:4096], T[3][:, 0:1024], [TD[3]], [RBD[3]], en="act")
        QTb, KTb = ATT[:, 0:1024], ATT[:, 1024:2048]
        KDb, VVb = ATT[:, 2048:3072], ATT[:, 3072:4096]
        LD = [Dep("qtb"), Dep("ktb"), Dep("kdb"), Dep("vvb")]
        for tci in range(8):
            t0 = tci * 128
            K.dma("sp", QTb, rqt[l, tci], reads=[RQD], writes=[LD[0]])
            K.dma("sp", KTb, rkt[l, tci], reads=[RQD], writes=[LD[1]])
            K.dma("sp", KDb, rkd[l, tci], reads=[RQD], writes=[LD[2]])
            K.dma("sp", VVb, rvs[l, tci], reads=[RQD], writes=[LD[3]])
            for h in range(8):
                hs = slice(h * 128, (h + 1) * 128)
                pz = h % 2
                mm(PS[pz][:, 0:128], KTb[:, hs], QTb[:, hs], True, True, [LD[0], LD[1]], [PD[pz]])
                tt(SQ[pz][:, 0:128], PS[pz][:, 0:128], M01[:], ALU.mult, [PD[pz], CD], [SQD[pz]])
                po = PS[2 + h // 4][:, (h % 4) * 128:(h % 4 + 1) * 128]
                mm(po, SQ[pz][:, 0:128], VVb[:, hs], True, False, [SQD[pz], LD[3]], [PD[2 + h // 4]])
                mm(po, QTb[:, hs], SBF[:, h, :], False, True, [LD[0], RBD[3]], [PD[2 + h // 4]])
                pd_ = 4 + (h % 2)
                mm(PS[pd_][:, 0:128], KDb[:, hs], VVb[:, hs], True, True, [LD[2], LD[3]], [PD[pd_]])
                stt(SF[:, h, :], SF[:, h, :], float(GAMMA[h] ** 128), PS[pd_][:, 0:128], ALU.mult, ALU.add, [PD[pd_]], [TD[3]])
                cp(SBF[:, h, :], SF[:, h, :], [TD[3]], [RBD[3]], en="act")
            groupnorm_chunk(tci, t0, 128)
        K.dma("sp", rs_out[l].rearrange("h d e -> d h e"), SF, reads=[TD[3]])
        SS = T[0][:, 0:1024].rearrange("p (h e) -> p h e", h=8)
        OHv = OH[:].rearrange("p (s t) -> p s t", s=8)
        for s in range(8):
            K.dma("sp", SS, st_ret[l, s].rearrange("h d e -> d h e"), writes=[TD[0]])
            ts(T[1][:8, 0:1024], SV[:8, :], IDF[:8, s:s + 1], ALU.mult, [SVD, CD], [TD[1]])
            QM = SM[:, 0:64].rearrange("p (h t) -> p h t", h=8)
            tt(QM, SQT, OHv[:, s, :].unsqueeze(1).to_broadcast([128, 8, 8]), ALU.mult, [SQTD, CD], [SMD])
            for h in range(8):
                hs = slice(h * 128, (h + 1) * 128)
                pd_ = 4 + (h % 2)
                mm(PS[pd_][:, 0:128], SK[:8, hs], T[1][:8, hs], True, True, [SKD, TD[1]], [PD[pd_]])
                stt(SS[:, h, :], SS[:, h, :], float(GAMMA[h]), PS[pd_][:, 0:128], ALU.mult, ALU.add, [PD[pd_]], [TD[0]])
            K.dma("sp", rss_out[l, s].rearrange("h d e -> d h e"), SS, reads=[TD[0]])
            for h in range(8):
                po = PS[2 + h // 4][:8, (h % 4) * 128:(h % 4 + 1) * 128]
                mm(po, QM[:, h, :], SS[:, h, :], True, True, [SMD, TD[0]], [PD[2 + h // 4]])
            for hf_ in range(2):
                osl = T[2][:8, hf_ * 512:(hf_ + 1) * 512]
                if s == 0:
                    cp(osl, PS[2 + hf_][:8, :], [PD[2 + hf_]], [TD[2]])
                else:
                    tt(osl, osl, PS[2 + hf_][:8, :], ALU.add, [PD[2 + hf_]], [TD[2]])
        groupnorm_chunk(8, 1024, 8, [T[2][:, 0:512], T[2][:, 512:1024]], [TD[2], TD[2]])
    def mixer_C(l):
        HALO = UBf[:, 48:64].rearrange("p (j c) -> p j c", j=2)
        for j in range(2):
            K.dma("sp", HALO[:, j, :], cc2_out[l][1024 + j * 8:1032 + j * 8, :].rearrange("c p -> p c"), reads=[CC2O], writes=[UBD],
                  more=(j == 1))
        ts(UBf[:, 48:64], UBf[:, 48:64], FLAG[:, 0:1], ALU.mult, [CD], [UBD])
        SPREV = UBf[:, 192:320].rearrange("p (c t) -> p c t", c=8)
        CSS = UBf[:, 320:448].rearrange("p (c t) -> p c t", c=8)
        K.dma("sp", T[0][:16, 0:1024], st_conv[l], writes=[TD[0]])
        for c in range(8):
            tr(PS[6][:, c * 16:(c + 1) * 16], T[0][:16, c * 128:(c + 1) * 128], IDF[:16, :16], [TD[0], CD], [PD[6]])
        cp(SPREV, PS[6][:, 0:128].rearrange("p (c t) -> p c t", c=8), [PD[6]], [UBD])
        for c in range(8):
            sa, sad = load_slab(w_in[l], 0, 16, 3072 + c * 128, 128)
            load_slab(w_in[l], 0, 16, 5120 + c * 128, 128, off=2048, more=True)
            sa2 = WS[slab_i[0]][:, 2048:4096].rearrange("p (k n) -> p k n", k=16)
            sbv, sbd = load_slab(w_in[l], 0, 16, 4096 + c * 128, 128)
            for (c0, n) in CTS:
                for kc in range(16):
                    mm(PS[0][:, :n], sa[:, kc, :], XNv[:, kc, c0:c0 + n], kc == 0, kc == 15, [sad, XND], [PD[0]])
                for kc in range(16):
                    mm(PS[1][:, :n], sa2[:, kc, :], XNv[:, kc, c0:c0 + n], kc == 0, kc == 15, [sad, XND], [PD[1]])
                for kc in range(16):
                    mm(PS[2][:, :n], sbv[:, kc, :], XNv[:, kc, c0:c0 + n], kc == 0, kc == 15, [sbd, XND], [PD[2]])
                cp(T[0][:, :n], PS[0][:, :n], [PD[0]], [TD[0]], en="act")
                tt(UB[:, 2 + c0:2 + c0 + n], T[0][:, :n], PS[1][:, :n], ALU.mult, [TD[0], PD[1]], [UBD])
                cp(T[1][:, c0:c0 + n], PS[2][:, :n], [PD[2]], [TD[1]], en="act")
            cp(UB[:, 0:2], HALO[:, :, c], [UBD], [UBD])
            w0, w1, w2 = [CONVW[:, l * 24 + c * 3 + j:l * 24 + c * 3 + j + 1] for j in range(3)]
            ts(T[2][:, 0:1024], UB[:, 2:1026], w2, ALU.mult, [UBD, CD], [TD[2]])
            stt(T[2][:, 0:1024], UB[:, 1:1025], w1, T[2][:, 0:1024], ALU.mult, ALU.add, [UBD, CD], [TD[2]])
            stt(T[2][:, 0:1024], UB[:, 0:1024], w0, T[2][:, 0:1024], ALU.mult, ALU.add, [UBD, CD], [TD[2]])
            sp3 = SPREV[:, c, :].rearrange("p (s j) -> p s j", s=8)
            ts(T[2][:, 1024:1032], UB[:, 1026:1034], w2, ALU.mult, [UBD, CD], [TD[2]])
            stt(T[2][:, 1024:1032], sp3[:, :, 1], w1, T[2][:, 1024:1032], ALU.mult, ALU.add, [UBD, CD], [TD[2]])
            stt(T[2][:, 1024:1032], sp3[:, :, 0], w0, T[2][:, 1024:1032], ALU.mult, ALU.add, [UBD, CD], [TD[2]])
            tt(YBv[:, c, :], T[1][:, 0:NT], T[2][:, 0:NT], ALU.mult, [TD[1], TD[2]], [YBD])
            cs3 = CSS[:, c, :].rearrange("p (s j) -> p s j", s=8)
            cp(cs3[:, :, 0], sp3[:, :, 1], [UBD], [UBD])
            cp(cs3[:, :, 1], UB[:, 1026:1034], [UBD], [UBD])
        for c in range(8):
            pb = 6 + c // 4
            tr(PS[pb][:16, (c % 4) * 128:(c % 4 + 1) * 128], CSS[:, c, :], IDF[:, :], [UBD, CD], [PD[pb]])
        for hf_ in range(2):
            cp(T[0][:16, hf_ * 512:(hf_ + 1) * 512], PS[6 + hf_][:16, :], [PD[6 + hf_]], [TD[0]])
        K.dma("sp", css_out[l], T[0][:16, 0:1024], reads=[TD[0]])
        if stop_after == "mixC1":
            return
        MGB = BIG[:, 2064:2064 + 2 * NT].rearrange("p (o t) -> p o t", o=2)
        MGBD = Dep("mgb")
        Ys = [(YAv, YAD), (YBv, YBD), (YCv, YCD)]
        for og in range(8):
            for br in range(3):
                gs, gsd = load_slab(w_in[l], 0, 16, 9216 + br * 2048 + og * 256, 256)
                bs, bsd = load_slab(w_br[br][l], 0, 8, og * 256, 256)
                Yv, Yd = Ys[br]
                for o2 in range(2):
                    for (c0, n) in CTS:
                        for kc in range(16):
                            mm(PS[0][:, :n], gs[:, kc, o2 * 128:(o2 + 1) * 128], XNv[:, kc, c0:c0 + n], kc == 0, kc == 15, [gsd, XND], [PD[0]])
                        for kc in range(8):
                            mm(PS[1][:, :n], bs[:, kc, o2 * 128:(o2 + 1) * 128], Yv[:, kc, c0:c0 + n], kc == 0, kc == 7, [bsd, Yd], [PD[1]])
                        act(T[0][:, :n], PS[0][:, :n], AF.Sigmoid, [PD[0]], [TD[0]])
                        M_ = T[2 + o2][:, c0:c0 + n]
                        Md = TD[2 + o2]
                        if br == 0:
                            tt(M_, T[0][:, :n], PS[1][:, :n], ALU.mult, [TD[0], PD[1]], [Md])
                        else:
                            tt(T[1][:, :n], T[0][:, :n], PS[1][:, :n], ALU.mult, [TD[0], PD[1]], [TD[1]])
                            if br == 1:
                                tt(M_, M_, T[1][:, :n], ALU.add, [TD[1]], [Md])
                            else:
                                tt(MGB[:, o2, c0:c0 + n], M_, T[1][:, :n], ALU.add, [TD[1], Md], [MGBD])
            for o2 in range(2):
                K.dma("sp", mg_scr[l, og * 2 + o2], MGB[:, o2, :], reads=[MGBD], writes=[MGD], more=True)

    def outproj_ple(l):
        K.dma("sp", XNv, mg_scr[l].rearrange("c p t -> p c t"), reads=[MGD], writes=[XND])
        it = 0
        for og in range(8):
            sv_, sdp = load_slab(w_out[l], 0, 16, og * 256, 256)
            for o2 in range(2):
                oc = og * 2 + o2
                for (c0, n) in CTS:
                    pz = it % 2
                    it += 1
                    for kc in range(16):
                        mm(PS[pz][:, :n], sv_[:, kc, o2 * 128:(o2 + 1) * 128], XNv[:, kc, c0:c0 + n], kc == 0, kc == 15, [sdp, XND], [PD[pz]])
                    tt(Xv[:, oc, c0:c0 + n], Xv[:, oc, c0:c0 + n], PS[pz][:, :n], ALU.add, [PD[pz]], [XD[oc]])

    def ple(l):
        rmsnorm(l * 4 + 3)
        PT = BIG[:, 0:2 * NT].rearrange("p (k t) -> p k t", k=2)
        for ti, (t0, r) in enumerate(TCS):
            K.dma("sp", T[ti % 2][:r, 0:256], p_in[l, t0:t0 + r, :], writes=[TD[ti % 2]])
            for kc in range(2):
                tr(PS[6][:, kc * 128:kc * 128 + r], T[ti % 2][:r, kc * 128:(kc + 1) * 128], IDF[:r, :r], [TD[ti % 2], CD], [PD[6]])
            cp(PT[:, :, t0:t0 + r], PS[6][:, 0:256].rearrange("p (k t) -> p k t", k=2)[:, :, :r], [PD[6]], [BIGD])
        for og in range(8):
            gs, gsd = load_slab(w_pg[l], 0, 16, og * 256, 256)
            us, usd = load_slab(w_pu[l], 0, 2, og * 256, 256)
            for o2 in range(2):
                oc = og * 2 + o2
                for (c0, n) in CTS:
                    for kc in range(16):
                        mm(PS[0][:, :n], gs[:, kc, o2 * 128:(o2 + 1) * 128], XNv[:, kc, c0:c0 + n], kc == 0, kc == 15, [gsd, XND], [PD[0]])
                    for kc in range(2):
                        mm(PS[1][:, :n], us[:, kc, o2 * 128:(o2 + 1) * 128], PT[:, kc, c0:c0 + n], kc == 0, kc == 1, [usd, BIGD], [PD[1]])
                    act(T[2][:, :n], PS[0][:, :n], AF.Sigmoid, [PD[0]], [TD[2]])
                    tt(T[3][:, :n], T[2][:, :n], PS[1][:, :n], ALU.mult, [TD[2], PD[1]], [TD[3]])
                    tt(Xv[:, oc, c0:c0 + n], Xv[:, oc, c0:c0 + n], T[3][:, :n], ALU.add, [TD[3]], [XD[oc]])

    cc1k_in_t = [dscr("cc1ki%d" % l, [1024, 512], F32) for l in range(L)]
    cc1k_out_t = [dscr("cc1ko%d" % l, [2048, 512], F32) for l in range(L)]
    cc1v_in_t = [dscr("cc1vi%d" % l, [1024, 1024], BF16) for l in range(L)]
    cc1v_out_t = [dscr("cc1vo%d" % l, [2048, 1024], BF16) for l in range(L)]
    cc2_in_t = [dscr("cc2i%d" % l, [1040, 128], F32) for l in range(L)]
    cc2_out_t = [dscr("cc2o%d" % l, [2080, 128], F32) for l in range(L)]
    cc3_in_t = [dscr("cc3i%d" % l, [NS, 128], F32) for l in range(L)]
    cc3_out_t = [dscr("cc3o%d" % l, [8 * NS, 128], F32) for l in range(L)]
    cc1k_in = [t.ap().bitcast(BF16) for t in cc1k_in_t]
    cc1k_out = [t.ap().bitcast(BF16) for t in cc1k_out_t]
    cc1v_in = [t.ap() for t in cc1v_in_t]
    cc1v_out = [t.ap() for t in cc1v_out_t]
    cc2_in = [t.ap() for t in cc2_in_t]
    cc2_out = [t.ap() for t in cc2_out_t]
    cc3_in = [t.ap() for t in cc3_in_t]
    cc3_out = [t.ap() for t in cc3_out_t]
    qt_scr, rqt, rkt, rkd, rktot, rvs, mg_scr, qs_scr = [t.ap() for t in (qt_scr, rqt, rkt, rkd, rktot, rvs, mg_scr, qs_scr)]
    x_scr_ap = x_scr.ap()
    for l in range(L):
        if stop_after == "load":
            break
        rmsnorm(l * 4 + 0)
        ffn(l, 0)
        if stop_after == "ffn1":
            break
        if not (ENABLE_MIXER or stop_after):
            rmsnorm(l * 4 + 2)
            ffn(l, 1)
            continue
        rmsnorm(l * 4 + 1)
        K.dma("sp", x_scr_ap[:, :], X[:], reads=XD, writes=[XSD])
        barrier()
        mixer_A(l)
        barrier()
        if stop_after not in ("mixA", "mixA0", "mixA1", "mixA2", "mixA3"):
            mixer_B(l)
            barrier()
        if stop_after not in ("mixA", "mixA0", "mixA1", "mixA2", "mixA3", "mixB2", "mixB3", "mixB1"):
            mixer_C(l)
            barrier()
        if stop_after in ("mixA", "mixA0", "mixA1", "mixA2", "mixA3", "mixB2", "mixB3", "mixB1", "mixC1"):
            K.dma("sp", X[:], x_scr_ap[:, :], reads=[XSD], writes=XD)
            barrier()
            break
        K.dma("sp", X[:], x_scr_ap[:, :], reads=[XSD], writes=XD)
        barrier()
        outproj_ple(l)
        rmsnorm(l * 4 + 2)
        ffn(l, 1)
        ple(l)

    barrier()
    for ti, (t0, r) in enumerate(TCS):
        stg = BIGF[:, (ti % 2) * 2048:(ti % 2) * 2048 + 2048]
        sd = SQD[ti % 2]
        for g in range(4):
            pb = (ti * 4 + g) % 4
            for j in range(4):
                fc = g * 4 + j
                tr(PS[pb][:r, j * 128:(j + 1) * 128], Xv[:, fc, t0:t0 + r], IDF[:, :], [XD[fc], CD], [PD[pb]])
            cp(stg[:r, g * 512:(g + 1) * 512], PS[pb][:r, :], [PD[pb]], [sd], en=("act" if g % 2 else "dve"))
        K.dma("sp", y_out[t0:t0 + r, :], stg[:r, :], reads=[sd])
    barrier()
    nc._ktrace = K.trace
    return nc


def simulate(trace):
    sems = {}
    pcs = {k: 0 for k in trace}
    progress = True
    while progress:
        progress = False
        for k, tr_ in trace.items():
            while pcs[k] < len(tr_):
                kind, sid, val, tag = tr_[pcs[k]]
                if kind == "wait":
                    if sems.get(sid, 0) >= val:
                        pcs[k] += 1
                        progress = True
                    else:
                        break
                else:
                    sems[sid] = sems.get(sid, 0) + val
                    pcs[k] += 1
                    progress = True
    for k, tr_ in trace.items():
        if pcs[k] < len(tr_):
            print("BLOCKED", k, "at", pcs[k], "/", len(tr_), tr_[pcs[k]], "cur", sems.get(tr_[pcs[k]][1], 0))
    return all(pcs[k] == len(trace[k]) for k in trace)


def _bf16(a):
    return np.asarray(a, np.float32).astype(ml_dtypes.bfloat16)


def make_consts(hf):
    c = {}
    c["c_idf"] = np.eye(128, dtype=np.float32)
    c["c_idb"] = _bf16(np.eye(128))
    c["c_ones"] = _bf16(np.ones((128, 128)))
    j = np.arange(128)[:, None]
    i = np.arange(128)[None, :]
    c["c_m01"] = (i >= j).astype(np.float32)
    q = np.arange(512)[None, :]
    c["c_msb"] = np.concatenate([((r * 128 + j) < q).astype(np.float32) for r in range(4)], axis=1)
    c["c_nu"] = _bf16(-(j > i).astype(np.float32))
    c["c_nl"] = _bf16(-(j <= i).astype(np.float32))
    c["c_usf"] = (j > i).astype(np.float32)
    half = 64
    inv = (10000.0 ** (-np.arange(half, dtype=np.float32) / half)).astype(np.float32)
    pos = np.zeros((128, 9), np.float32)
    for tc in range(8):
        pos[:, tc] = hf * 1024 + tc * 128 + np.arange(128)
    pos[:, 8] = PAST
    ang = (pos[:, :, None] * inv[None, None, :]).astype(np.float32)
    c["c_cos"] = np.cos(ang).astype(np.float32).reshape(128, 9 * 64)
    c["c_sin"] = np.sin(ang).astype(np.float32).reshape(128, 9 * 64)
    lg = np.log1p(-np.exp2(-5.0 - np.arange(H, dtype=np.float64)))
    p = np.arange(128, dtype=np.float64)[:, None, None]
    tcs = np.arange(9, dtype=np.float64)[None, :, None]
    lgh = lg[None, None, :]
    sc = 128.0 ** -0.5
    dinv = np.exp(-lgh * (p + 1.0)) * sc * np.ones_like(tcs)
    ddec = np.exp(lgh * (127.0 - p)) * sc * np.ones_like(tcs)
    dtot = np.exp(lgh * (1023.0 - (tcs * 128 + p))) * sc
    dq = np.exp(lgh * (p + 1.0)) * np.ones_like(tcs)
    dinv[:, 8, :] = sc
    ddec[:, 8, :] = sc
    dtot[:, 8, :] = sc
    dq[:, 8, :] = 1.0
    c["c_dec"] = np.stack([dinv, ddec, dtot, dq], axis=1).astype(np.float32).reshape(128, 4 * 72)
    oh = np.zeros((128, 8, 8), np.float32)
    for s in range(8):
        oh[:, s, s] = 1.0
    c["c_oh"] = oh.reshape(128, 64)
    return c


_NC_CACHE = {}


def kernel(x_prompt, x_sample, cache_sb_k, cache_sb_v, state_ret, state_conv, page_table, p_prompt, p_sample,
           ffn1_norm, ffn1_w_gu, ffn1_w_down, mix_norm, w_in, ret_gn, conv_w, sb_q_norm, sb_k_norm, sb_bias,
           w_branch_ret, w_branch_conv, w_branch_sb, w_out, ffn2_norm, ffn2_w_gu, ffn2_w_down,
           ple_norm, w_ple_gate, w_ple_up):
    f = lambda a: np.ascontiguousarray(np.asarray(a))
    x_prompt, x_sample = f(x_prompt), f(x_sample)
    if "nc" not in _NC_CACHE:
        _NC_CACHE["nc"] = build(STOP_AFTER)
    nc = _NC_CACHE["nc"]
    gam = np.stack([f(ffn1_norm), f(mix_norm), f(ffn2_norm), f(ple_norm)], axis=1)
    gam = np.ascontiguousarray(gam.reshape(L * 4, 16, 128).transpose(2, 0, 1).reshape(128, L * 64))
    convw = np.ascontiguousarray(f(conv_w).reshape(L, 3, 8, 128).transpose(3, 0, 2, 1).reshape(128, L * 24))
    shared = {
        "ffn1_w_gu": f(ffn1_w_gu), "ffn2_w_gu": f(ffn2_w_gu), "ffn1_w_down": f(ffn1_w_down), "ffn2_w_down": f(ffn2_w_down),
        "w_in": f(w_in), "w_branch_ret": f(w_branch_ret), "w_branch_conv": f(w_branch_conv), "w_branch_sb": f(w_branch_sb),
        "w_out": f(w_out), "w_ple_gate": f(w_ple_gate), "w_ple_up": f(w_ple_up), "gam": gam,
        "ret_gn": f(ret_gn).reshape(L, 1024), "sb_q_norm": f(sb_q_norm), "sb_k_norm": f(sb_k_norm), "sb_bias": f(sb_bias),
        "convw": convw, "st_ret": f(state_ret), "st_conv": f(state_conv).reshape(L, NS * 2, 1024),
        "ptab": f(page_table).astype(np.int32),
    }
    ck_all, cv_all = np.asarray(cache_sb_k), np.asarray(cache_sb_v)
    psamp = f(p_sample).reshape(L, NS, 256)
    in_maps = []
    for c in range(8):
        b, hf = c // 2, c % 2
        m = dict(shared)
        m["x_in"] = np.concatenate([x_prompt[b, hf * NP:(hf + 1) * NP], x_sample[:, 0, :]], axis=0)
        m["p_in"] = np.concatenate([f(p_prompt)[:, b, hf * NP:(hf + 1) * NP], psamp], axis=1)
        m["ck"] = np.ascontiguousarray(ck_all[:, :, :, c, :])
        m["cv"] = np.ascontiguousarray(cv_all[:, :, :, c, :])
        m["flag"] = np.full((128, 1), float(hf), np.float32)
        ohc = np.zeros((128, 8), np.float32)
        ohc[:, c] = 1.0
        m["ohc"] = ohc
        m.update(make_consts(hf))
        in_maps.append(m)
    if DEBUG_SMALL:
        for m in in_maps:
            for k_ in ("ffn1_w_gu", "ffn2_w_gu", "ffn1_w_down", "ffn2_w_down"):
                m[k_] = np.zeros((L, 1, 1), np.float32)
            m["ck"] = m["ck"][:, :8]
            m["cv"] = m["cv"][:, :8]
            if DEBUG_SMALL == 2:
                for k_ in ("w_branch_ret", "w_branch_conv", "w_branch_sb", "w_out", "w_ple_gate"):
                    m[k_] = np.zeros((L, 1, 1), np.float32)
    res = run_bass_kernel_spmd(nc, in_maps, core_ids=list(range(8))).results
    B, S = 4, 2048
    yp = np.zeros((B, S, D), np.float32)
    nkp = np.zeros((L, B, S, H, DH), np.float32)
    nvp = np.zeros((L, B, S, H, DH), np.float32)
    rsp = np.zeros((L, B, H, DH, DH), np.float32)
    csp = np.zeros((L, B, 2, 1024), np.float32)
    for c in range(8):
        b, hf = c // 2, c % 2
        r = res[c]
        yp[b, hf * NP:(hf + 1) * NP] = r["y"][:NP]
        nkp[:, b, hf * NP:(hf + 1) * NP] = r["nk"][:, :NP].reshape(L, NP, H, DH)
        nvp[:, b, hf * NP:(hf + 1) * NP] = r["nv"][:, :NP].reshape(L, NP, H, DH)
        if hf == 1:
            rsp[:, b] = r["rs"]
            csp[:, b] = r["cs"]
    r0 = res[0]
    ys = r0["y"][NP:].reshape(NS, 1, D).copy()
    nks = r0["nk"][:, NP:].reshape(L, NS, 1, H, DH).copy()
    nvs = r0["nv"][:, NP:].reshape(L, NS, 1, H, DH).copy()
    rss = r0["rss"].copy()
    css = r0["css"].reshape(L, NS, 2, 1024).copy()
    return (yp, ys, nkp, nvp, nks, nvs, rsp, rss, csp, css)
```
